# Optimizing a Trainium2 kernel written in Bass

```python
import jax, jax.numpy as jnp
from jax import lax
import numpy as np

D_MODEL = 2048
BATCH = 16
SEQ = 2048
DEPTH = 4

D_MIX = D_MODEL
HG_HEADS = 8
HG_KEY_DIM = 128
HG_VAL_DIM = 128
HG_QK = HG_HEADS * HG_KEY_DIM
HG_WIDTH = HG_HEADS * HG_VAL_DIM
HG_CHUNK = 32
MLA_HEADS = 8
MLA_NOPE = 128
MLA_ROPE = 64
MLA_V = 128
MLA_Q_RANK = 512
MLA_KV_RANK = 256
MLA_WIDTH = MLA_HEADS * MLA_V
Q_BLOCK = 128
ROPE_THETA = 10000.0
IN_SIZES = (HG_QK, HG_QK, HG_WIDTH, HG_WIDTH, MLA_Q_RANK, MLA_KV_RANK, MLA_ROPE)
D_IN = HG_QK * 2 + HG_WIDTH * 2 + MLA_Q_RANK + MLA_KV_RANK + MLA_ROPE
D_FF = 5632
N_SUB = 3
EPS = 1e-6

kernel_name = "hymba_hgrn2_mla_macaron_adaln"


def rmsnorm(x, w):
    xf = x.astype(jnp.float32)
    y = xf * lax.rsqrt(jnp.mean(xf * xf, axis=-1, keepdims=True) + EPS)
    return (y * w.astype(jnp.float32)).astype(x.dtype)


def modulate(x, g, shift, scale):
    return rmsnorm(x, g) * (1 + scale[:, None, :]) + shift[:, None, :]


def swiglu(h, w_gate, w_up, w_down):
    return (jax.nn.silu(h @ w_gate) * (h @ w_up)) @ w_down


def apply_rope(x, cos, sin):
    x1, x2 = jnp.split(x, 2, axis=-1)
    xf1, xf2 = x1.astype(jnp.float32), x2.astype(jnp.float32)
    return jnp.concatenate([xf1 * cos - xf2 * sin, xf1 * sin + xf2 * cos], axis=-1).astype(x.dtype)


def hgrn2_chunkwise(q, k, v, log_f):
    B, S, H, DK = q.shape
    DV = v.shape[-1]
    n_chunks = S // HG_CHUNK

    def to_chunks(t):
        return t.astype(jnp.float32).reshape(B, n_chunks, HG_CHUNK, H, t.shape[-1]).transpose(1, 0, 3, 2, 4)

    q, k, v, log_f = to_chunks(q), to_chunks(k), to_chunks(v), to_chunks(log_f)
    b = jnp.cumsum(log_f, axis=-2)
    b_end = b[..., -1:, :]
    q_in = q * jnp.exp(b)
    k_in = k * jnp.exp(-b)
    k_out = k * jnp.exp(b_end - b)
    decay_chunk = jnp.exp(b_end[..., 0, :])
    causal = jnp.tril(jnp.ones((HG_CHUNK, HG_CHUNK), dtype=bool))
    a = jnp.einsum('nbhtd,nbhsd->nbhts', q_in, k_in)
    a = jnp.where(causal, a, 0.0)
    o_intra = jnp.einsum('nbhts,nbhsv->nbhtv', a, v)

    def step(state, xs):
        qi, ko, vc, dc = xs
        o_inter = jnp.einsum('bhtd,bhdv->bhtv', qi, state)
        state = state * dc[..., None] + jnp.einsum('bhsd,bhsv->bhdv', ko, vc)
        return state, o_inter

    state0 = jnp.zeros((B, H, DK, DV), jnp.float32)
    _, o_inter = lax.scan(step, state0, (q_in, k_out, v, decay_chunk))
    o = o_intra + o_inter
    return o.transpose(1, 0, 3, 2, 4).reshape(B, S, H, DV)


def causal_block_attention(q, k, v):
    S = q.shape[1]
    scale = (MLA_NOPE + MLA_ROPE) ** -0.5
    outs = []
    for blk in range(S // Q_BLOCK):
        lo, hi = blk * Q_BLOCK, (blk + 1) * Q_BLOCK
        qb, kb, vb = q[:, lo:hi], k[:, :hi], v[:, :hi]
        s = jnp.einsum('bqhd,bkhd->bhqk', qb, kb).astype(jnp.float32) * scale
        mask = jnp.arange(hi)[None, :] <= (lo + jnp.arange(Q_BLOCK))[:, None]
        s = jnp.where(mask, s, -jnp.inf)
        p = jax.nn.softmax(s, axis=-1).astype(vb.dtype)
        outs.append(jnp.einsum('bhqk,bkhv->bqhv', p, vb))
    return jnp.concatenate(outs, axis=1)


def token_mix(h, cos, sin, lb, w_in, qa_norm_g, w_q_up, kva_norm_g, w_kv_up, hg_norm_g, w_out):
    B, S, _ = h.shape
    proj = h @ w_in
    split_points = []
    acc = 0
    for size in IN_SIZES[:-1]:
        acc += size
        split_points.append(acc)
    hq, hf, hi, hg, qa, kva, kpe = jnp.split(proj, split_points, axis=-1)

    q = jax.nn.silu(hq).reshape(B, S, HG_HEADS, HG_KEY_DIM)
    z = hf.astype(jnp.float32).reshape(B, S, HG_HEADS, HG_KEY_DIM)
    lb = lb.reshape(HG_HEADS, HG_KEY_DIM)
    log_f = jnp.logaddexp(jnp.log(lb), jnp.log1p(-lb) + jax.nn.log_sigmoid(z))
    k = (1.0 - lb) * jax.nn.sigmoid(-z)
    v = hi.reshape(B, S, HG_HEADS, HG_VAL_DIM)
    o_hg = hgrn2_chunkwise(q, k, v, log_f).astype(h.dtype)
    o_hg = rmsnorm(o_hg, hg_norm_g) * jax.nn.silu(hg.reshape(B, S, HG_HEADS, HG_VAL_DIM))
    o_hg = o_hg.reshape(B, S, HG_WIDTH)

    cq = rmsnorm(qa, qa_norm_g)
    qh = (cq @ w_q_up).reshape(B, S, MLA_HEADS, MLA_NOPE + MLA_ROPE)
    q_nope, q_pe = qh[..., :MLA_NOPE], qh[..., MLA_NOPE:]
    ckv = rmsnorm(kva, kva_norm_g)
    kvh = (ckv @ w_kv_up).reshape(B, S, MLA_HEADS, MLA_NOPE + MLA_V)
    k_nope, v_m = kvh[..., :MLA_NOPE], kvh[..., MLA_NOPE:]
    q_pe = apply_rope(q_pe, cos, sin)
    k_pe = apply_rope(kpe[:, :, None, :], cos, sin)
    q_full = jnp.concatenate([q_nope, q_pe], axis=-1)
    k_full = jnp.concatenate([k_nope, jnp.broadcast_to(k_pe, (B, S, MLA_HEADS, MLA_ROPE))], axis=-1)
    o_mla = causal_block_attention(q_full, k_full, v_m).reshape(B, S, MLA_WIDTH)

    return jnp.concatenate([o_hg, o_mla], axis=-1) @ w_out


def setup_inputs(seed: int = 0) -> dict:
    key = jax.random.key(seed)
    ks = jax.random.split(key, 20)
    f32 = jnp.float32

    def nrm(k, shape, fan_in, mult=1.0):
        return jax.random.normal(k, shape, f32) * (mult * fan_in ** -0.5)

    x = jax.random.normal(ks[0], (BATCH, SEQ, D_MODEL), f32)
    c = jax.random.normal(ks[1], (BATCH, D_MODEL), f32)
    offsets = jax.random.randint(ks[2], (BATCH, 1), 0, 4096, dtype=jnp.int32)
    positions = (offsets + jnp.arange(SEQ, dtype=jnp.int32)[None, :]).astype(jnp.int32)
    w_ada = nrm(ks[3], (DEPTH, D_MODEL, N_SUB * 3 * D_MODEL), D_MODEL, 0.5)
    b_ada = 0.02 * jax.random.normal(ks[4], (DEPTH, N_SUB * 3 * D_MODEL), f32)
    norm_g = 1.0 + 0.02 * jax.random.normal(ks[5], (DEPTH, N_SUB, D_MODEL), f32)
    w_in = nrm(ks[6], (DEPTH, D_MODEL, D_IN), D_MODEL)
    qa_norm_g = 1.0 + 0.02 * jax.random.normal(ks[7], (DEPTH, MLA_Q_RANK), f32)
    w_q_up = nrm(ks[8], (DEPTH, MLA_Q_RANK, MLA_HEADS * (MLA_NOPE + MLA_ROPE)), MLA_Q_RANK)
    kva_norm_g = 1.0 + 0.02 * jax.random.normal(ks[9], (DEPTH, MLA_KV_RANK), f32)
    w_kv_up = nrm(ks[10], (DEPTH, MLA_KV_RANK, MLA_HEADS * (MLA_NOPE + MLA_V)), MLA_KV_RANK)
    hg_lb_logits = jax.random.normal(ks[11], (DEPTH, HG_QK), f32)
    hg_norm_g = 1.0 + 0.02 * jax.random.normal(ks[12], (DEPTH, HG_VAL_DIM), f32)
    w_out = nrm(ks[13], (DEPTH, D_MIX, D_MODEL), D_MIX)
    ffn_w_gate = nrm(ks[14], (DEPTH, 2, D_MODEL, D_FF), D_MODEL)
    ffn_w_up = nrm(ks[15], (DEPTH, 2, D_MODEL, D_FF), D_MODEL)
    ffn_w_down = nrm(ks[16], (DEPTH, 2, D_FF, D_MODEL), D_FF)
    final_norm_g = 1.0 + 0.02 * jax.random.normal(ks[17], (D_MODEL,), f32)
    return {"x": x, "c": c, "positions": positions, "w_ada": w_ada, "b_ada": b_ada,
            "norm_g": norm_g, "w_in": w_in, "qa_norm_g": qa_norm_g, "w_q_up": w_q_up,
            "kva_norm_g": kva_norm_g, "w_kv_up": w_kv_up, "hg_lb_logits": hg_lb_logits,
            "hg_norm_g": hg_norm_g, "w_out": w_out, "ffn_w_gate": ffn_w_gate,
            "ffn_w_up": ffn_w_up, "ffn_w_down": ffn_w_down, "final_norm_g": final_norm_g}


def reference(x, c, positions, w_ada, b_ada, norm_g, w_in, qa_norm_g, w_q_up, kva_norm_g,
              w_kv_up, hg_lb_logits, hg_norm_g, w_out, ffn_w_gate, ffn_w_up, ffn_w_down,
              final_norm_g):
    B = x.shape[0]
    half = MLA_ROPE // 2
    inv_freq = ROPE_THETA ** (-jnp.arange(half, dtype=jnp.float32) / half)
    ang = positions.astype(jnp.float32)[..., None] * inv_freq
    cos = jnp.cos(ang)[:, :, None, :]
    sin = jnp.sin(ang)[:, :, None, :]
    lb_all = jnp.cumsum(jax.nn.softmax(hg_lb_logits.astype(jnp.float32), axis=0), axis=0)
    lb_all = lb_all - lb_all[0:1]
    c_act = jax.nn.silu(c)

    for l in range(DEPTH):
        mod = (c_act @ w_ada[l] + b_ada[l]).reshape(B, N_SUB, 3, D_MODEL)
        h = modulate(x, norm_g[l, 0], mod[:, 0, 0], mod[:, 0, 1])
        x = x + 0.5 * mod[:, 0, 2][:, None, :] * swiglu(h, ffn_w_gate[l, 0], ffn_w_up[l, 0], ffn_w_down[l, 0])
        h = modulate(x, norm_g[l, 1], mod[:, 1, 0], mod[:, 1, 1])
        y = token_mix(h, cos, sin, lb_all[l], w_in[l], qa_norm_g[l], w_q_up[l], kva_norm_g[l],
                      w_kv_up[l], hg_norm_g[l], w_out[l])
        x = x + mod[:, 1, 2][:, None, :] * y
        h = modulate(x, norm_g[l, 2], mod[:, 2, 0], mod[:, 2, 1])
        x = x + 0.5 * mod[:, 2, 2][:, None, :] * swiglu(h, ffn_w_gate[l, 1], ffn_w_up[l, 1], ffn_w_down[l, 1])

    return rmsnorm(x, final_norm_g)
```

```python
import contextlib
import math
import numpy as np
import concourse.bass as bass
import concourse.mybir as mybir
from concourse.bass_utils import run_bass_kernel_spmd

F32 = mybir.dt.float32
BF16 = mybir.dt.bfloat16
I32 = mybir.dt.int32
AF = mybir.ActivationFunctionType
ALU = mybir.AluOpType

NCORES = 8
D = 2048
SEQ = 2048
DEPTH = 4
NSEQ = 2
TT = 512
NT = SEQ // TT
KC = D // 128
DFF = 5632
FC = DFF // 128
HH = 8
D_IN = 4928
EPS = 1e-6
NSLOT = 8
SLOTW = 2048
SCALE = (128 + 64) ** -0.5

N_FFN_T = 2 * FC + 16 * 4
N_MIX_T = 8 + 7 + 4 + 1 + 1 + 24 + 16
N_LAYER_T = 2 * N_FFN_T + N_MIX_T
N_ADA_T = 144


class T:
    __slots__ = ("ap", "w", "r", "name")

    def __init__(self, ap=None, name=""):
        self.ap = ap
        self.w = None
        self.r = []
        self.name = name


class Sched:
    CE = ("pe", "act", "dve", "pool")

    def __init__(self, nc, n_dma_sems=12):
        self.nc = nc
        self.ops = {e: [] for e in ("pe", "act", "dve", "pool", "sp")}
        self.seen = {e: {} for e in self.ops}
        self.dma_cnt = {}
        self.n_misc = n_dma_sems
        self.misc_rr = 0
        self.pending = {e: [] for e in self.ops}

    def _need(self, e, tok):
        if tok[0] == "e" and tok[1] == e and e == "pe":
            return False
        return self.seen[e].get(tok[1], -1) < tok[2]

    def _mark(self, e, tok):
        if self.seen[e].get(tok[1], -1) < tok[2]:
            self.seen[e][tok[1]] = tok[2]
        if tok[0] == "e":
            self.ops[tok[1]][tok[2]][2] = True

    def op(self, e, fn, reads=(), writes=(), dma_sem=None):
        deps = {}

        def add(tok):
            if tok is None:
                return
            k = tok[1]
            if k not in deps or deps[k][2] < tok[2]:
                deps[k] = tok
        for t in reads:
            add(t.w)
        for t in writes:
            add(t.w)
            for rt in t.r:
                add(rt)
        for tok in self.pending[e]:
            add(tok)
        self.pending[e] = []
        if dma_sem is not None:
            n = self.dma_cnt.get(dma_sem, 0)
            if n > 0:
                add(("d", dma_sem, 16 * n))
        waits = []
        for tok in deps.values():
            if self._need(e, tok):
                waits.append(tok)
                self._mark(e, tok)
        idx = len(self.ops[e])
        self.ops[e].append([fn, waits, False, dma_sem])
        if dma_sem is not None:
            n = self.dma_cnt.get(dma_sem, 0) + 1
            self.dma_cnt[dma_sem] = n
            mytok = ("d", dma_sem, 16 * n)
        else:
            mytok = ("e", e, idx)
        for t in reads:
            t.r.append(mytok)
            if len(t.r) > 48:
                d = {}
                for rt in t.r:
                    if rt[1] not in d or d[rt[1]][2] < rt[2]:
                        d[rt[1]] = rt
                t.r = list(d.values())
        for t in writes:
            t.w = mytok
            t.r = []
        return mytok

    def misc_sem(self):
        s = "m%d" % self.misc_rr
        self.misc_rr = (self.misc_rr + 1) % self.n_misc
        return s

    def dma(self, q, out_ap, in_ap, reads=(), writes=(), sem=None):
        if sem is None:
            sem = self.misc_sem()
        return self.op(q, lambda e, o=out_ap, i=in_ap: e.dma_start(out=o, in_=i),
                       reads=reads, writes=writes, dma_sem=sem)

    def barrier(self):
        toks = []
        for f in self.CE:
            for i in range(len(self.ops[f]) - 1, -1, -1):
                if self.ops[f][i][3] is None:
                    toks.append(("e", f, i))
                    break
        for s, n in self.dma_cnt.items():
            toks.append(("d", s, 16 * n))
        for e in self.ops:
            self.pending[e] = list(self.pending[e]) + toks

    def emit(self, final_waits_engine="sp"):
        nc = self.nc
        fin = [("d", s, 16 * n) for s, n in self.dma_cnt.items()]
        sem_names = list(self.CE) + sorted(self.dma_cnt.keys())
        with contextlib.ExitStack() as st:
            sems = {}
            for n in sem_names:
                sems[n] = st.enter_context(nc.semaphore("s_" + n))
            sigcnt = {}
            for e in self.CE:
                c = 0
                arr = []
                for o in self.ops[e]:
                    if o[2]:
                        c += 1
                    arr.append(c)
                sigcnt[e] = arr

            def resolve(tok):
                if tok[0] == "e":
                    return sems[tok[1]], sigcnt[tok[1]][tok[2]]
                return sems[tok[1]], tok[2]

            def replay(ename, eng):
                for fn, waits, sig, dsem in self.ops[ename]:
                    for tok in waits:
                        s, v = resolve(tok)
                        eng.wait_ge(s, v)
                    ins = fn(eng)
                    if dsem is not None:
                        ins.then_inc(sems[dsem], 16)
                    elif sig:
                        ins.then_inc(sems[ename], 1)
                if ename == final_waits_engine:
                    for tok in fin:
                        s, v = resolve(tok)
                        eng.wait_ge(s, v)

            block = st.enter_context(nc.Block())

            @block.tensor
            def _(e):
                replay("pe", e)

            @block.scalar
            def _(e):
                replay("act", e)

            @block.vector
            def _(e):
                replay("dve", e)

            @block.gpsimd
            def _(e):
                replay("pool", e)

            @block.sync
            def _(e):
                replay("sp", e)


def _chunk_tile(w, col0, ncols=128):
    K = w.shape[0]
    return w[:, col0:col0 + ncols].reshape(K // 128, 128, ncols).transpose(1, 0, 2)


def _pad_tile(t):
    t = np.ascontiguousarray(t).reshape(128, -1)
    out = np.zeros((128, SLOTW), np.float32)
    out[:, :t.shape[1]] = t
    return out


def ffn_tiles(wg, wu, wd):
    tiles = []
    for f in range(FC):
        tiles.append(_pad_tile(_chunk_tile(wg, f * 128)))
        tiles.append(_pad_tile(_chunk_tile(wu, f * 128)))
    wd4 = wd.reshape(4, 11, 128, D)
    for m in range(KC):
        for q in range(4):
            tiles.append(_pad_tile(wd4[q, :, :, m * 128:(m + 1) * 128].transpose(1, 0, 2)))
    return tiles


ROT = np.concatenate([np.arange(32, 64), np.arange(0, 32)])


def mixer_tiles(w_in, w_q_up, w_kv_up, w_out):
    tiles = []
    for cb in range(2):
        for kq in range(4):
            blk = w_in[kq * 512:(kq + 1) * 512, 2048 + cb * 512: 2048 + (cb + 1) * 512]
            tiles.append(_pad_tile(blk.reshape(4, 128, 512).transpose(1, 0, 2)))
    for c in range(4):
        tiles.append(_pad_tile(_chunk_tile(w_in, 4096 + c * 128)))
    for c in range(2):
        tiles.append(_pad_tile(_chunk_tile(w_in, 4608 + c * 128)))
    kpe = w_in[:, 4864:4928]
    both = np.concatenate([kpe, kpe[:, ROT]], axis=1)
    tiles.append(_pad_tile(_chunk_tile(both, 0)))
    wkv = w_kv_up.reshape(256, HH, 256)
    tiles.append(_pad_tile(wkv[:, :, :128].transpose(2, 1, 0)))
    wuv = wkv[:, :, 128:].reshape(2, 128, HH, 128).transpose(1, 0, 2, 3)
    tiles.append(_pad_tile(wuv))
    for hp in range(4):
        cols = []
        for hd in (2 * hp, 2 * hp + 1):
            base = hd * 192
            cols.append(w_q_up[:, base:base + 128])
            rp = w_q_up[:, base + 128:base + 192]
            cols.append(rp)
            cols.append(rp[:, ROT])
        blk = np.concatenate(cols, axis=1)
        tiles.append(_pad_tile(blk.reshape(4, 128, 512).transpose(1, 0, 2)))
    for hd in range(HH):
        tiles.append(_pad_tile(_chunk_tile(w_in, 0 + hd * 128)))
        tiles.append(_pad_tile(_chunk_tile(w_in, 1024 + hd * 128)))
        tiles.append(_pad_tile(_chunk_tile(w_in, 3072 + hd * 128)))
    for m in range(KC):
        tiles.append(_pad_tile(_chunk_tile(w_out, m * 128)))
    return tiles


def layer_stream(inp, l):
    tiles = ffn_tiles(inp["ffn_w_gate"][l, 0], inp["ffn_w_up"][l, 0], inp["ffn_w_down"][l, 0])
    tiles += mixer_tiles(inp["w_in"][l], inp["w_q_up"][l], inp["w_kv_up"][l], inp["w_out"][l])
    tiles += ffn_tiles(inp["ffn_w_gate"][l, 1], inp["ffn_w_up"][l, 1], inp["ffn_w_down"][l, 1])
    assert len(tiles) == N_LAYER_T
    return np.stack(tiles)


def ada_stream(w_ada_l):
    return np.ascontiguousarray(
        w_ada_l.reshape(KC, 128, N_ADA_T, 128).transpose(2, 1, 0, 3)).reshape(N_ADA_T, 128, SLOTW)


def make_consts():
    c = np.zeros((128, 2048), np.float32)
    p = np.arange(128)
    c[:, 0:128] = np.eye(128)
    c[:, 128:256] = (p[:, None] <= p[None, :])
    bd = (p[:, None] // 32 == p[None, :] // 32) & (p[:, None] <= p[None, :])
    c[:, 256:768] = np.tile(bd, (1, 4))
    sm = np.ones(512); sm[::32] = 0.0
    c[:, 768:1280] = sm[None, :]
    c[:, 1280:1284] = (p[:, None] // 32 == np.arange(4)[None, :])
    half = 32
    inv = (10000.0 ** (-np.arange(half, dtype=np.float32) / half)).astype(np.float32)
    c[:, 1284] = np.tile(inv, 4)
    sgn = np.where((p % 64) < 32, -1.0, 1.0)
    c[:, 1285] = sgn
    c[:, 1286] = -math.pi * sgn
    c[:, 1287] = EPS
    c[:, 1288] = -math.pi
    c[:, 1289] = 1.0
    return c


C_ID, C_TRI, C_BD, C_SM, C_CM, C_INV, C_SGN, C_SGNB, C_EPS, C_NPI, C_ONE = 0, 128, 256, 768, 1280, 1284, 1285, 1286, 1287, 1288, 1289


def build_program(n_layers, do_final, stages=("ffn1", "mix", "ffn2"), ntiles=NT, nseq=NSEQ):
    nc = bass.Bass("TRN2", target_bir_lowering=False)
    L = n_layers
    xT = nc.dram_tensor("xT", [NSEQ, D, SEQ], F32, kind="ExternalInput").ap()
    outT = nc.dram_tensor("outT", [NSEQ, D, SEQ], F32, kind="ExternalOutput").ap()
    cT = nc.dram_tensor("cT", [128, KC, NSEQ], F32, kind="ExternalInput").ap()
    posd = nc.dram_tensor("pos", [NSEQ, SEQ], I32, kind="ExternalInput").ap()
    wst = nc.dram_tensor("wst", [L, N_LAYER_T, 128, SLOTW], F32, kind="ExternalInput").ap()
    wada = nc.dram_tensor("wada", [L, N_ADA_T, 128, SLOTW], F32, kind="ExternalInput").ap()
    bada = nc.dram_tensor("bada", [L, 128, N_ADA_T], F32, kind="ExternalInput").ap()
    ngd = nc.dram_tensor("ng", [L, 128, 48], F32, kind="ExternalInput").ap()
    smalld = nc.dram_tensor("small", [L, 128, 16], F32, kind="ExternalInput").ap()
    lbd = nc.dram_tensor("lblog", [128, HH, DEPTH], F32, kind="ExternalInput").ap()
    lsel = nc.dram_tensor("lsel", [128, L * DEPTH], F32, kind="ExternalInput").ap()
    fng = nc.dram_tensor("fng", [128, KC], F32, kind="ExternalInput").ap()
    cstd = nc.dram_tensor("cst", [128, 2048], F32, kind="ExternalInput").ap()

    with contextlib.ExitStack() as st:
        def sb(name, shape, dt=F32):
            return st.enter_context(nc.sbuf_tensor(name, shape, dt))

        def pst(name, shape, dt=F32):
            return st.enter_context(nc.psum_tensor(name, shape, dt))

        S = Sched(nc)
        x_sb = sb("x_sb", [128, KC, TT])
        h_sb = sb("h_sb", [128, KC, TT], BF16)
        ring = sb("ring", [128, NSLOT, SLOTW], BF16)
        scr = sb("scr", [128, FC, TT], BF16)
        cst = sb("cst_sb", [128, 2048])
        cb16 = sb("cb16", [128, 1280], BF16)
        onesd = sb("onesd", [128, 4, 128], BF16)
        ones1 = sb("ones1", [128, 128], BF16)
        sq = sb("sq", [128, 2, TT], BF16)
        rstd = sb("rstd", [128, TT])
        tmpa = sb("tmpa", [128, 2, TT])
        sgt = sb("sgt", [128, 2, TT])
        modv = sb("modv", [128, L, 3, 3, KC, NSEQ])
        cact = sb("cact", [128, KC, NSEQ], BF16)
        c32 = sb("c32", [128, KC, NSEQ])
        ng_sb = sb("ng_sb", [128, L, 48])
        bada_sb = sb("bada_sb", [128, L, N_ADA_T])
        fng_sb = sb("fng_sb", [128, KC])
        ft = sb("ft", [128, 6, TT])
        on_t = sb("on_t", [128, TT])
        ropeC = sb("ropeC", [128, TT]); ropeS = sb("ropeS", [128, TT])
        posi = sb("posi", [128, TT], I32); ang = sb("ang", [128, TT]); angk = sb("angk", [128, TT]); angi = sb("angi", [128, TT], I32)
        rl = sb("rl", [128, TT])
        ckvT = sb("ckvT", [128, L, 2, SEQ], BF16)
        ckvtok = sb("ckvtok", [128, L, SEQ // 128, 256], BF16)
        kpeT = sb("kpeT", [128, L, SEQ], BF16)
        S32 = sb("S32", [128, L, HH, 128])
        Sbf = sb("Sbf", [128, L, HH, 128], BF16)
        dcs = sb("dcs", [128, HH, 16])
        lbv = sb("lbv", [128, L, 3, HH])
        lbl = sb("lbl", [128, HH, DEPTH]); lbe = sb("lbe", [128, HH, DEPTH]); lbm = sb("lbm", [128, HH]); lbt = sb("lbt", [128, HH, DEPTH])
        lsel_sb = sb("lsel_sb", [128, L * DEPTH]); small_sb = sb("small_sb", [128, L, 16])
        P = [pst("ps%d" % i, [128, TT]) for i in range(8)]
        P7b = P[7][:].bitcast(BF16)
        PT = [T(p, "ps%d" % i) for i, p in enumerate(P)]

        X = [T(name="x%d" % k) for k in range(KC)]
        H = [T(name="h%d" % k) for k in range(KC)]
        RT = [T(name="ring%d" % s) for s in range(NSLOT)]
        HID = [T(name="hid%d" % f) for f in range(FC)]
        Tc, Tcb, Tones, Tsq, Trstd = T(), T(), T(), [T(), T()], T()
        Ttmp = [T(), T()]
        Tsg = [T(), T()]
        Tmod, Tcact, Tc32, Tng, Tbada, Tfng = T(), T(), T(), T(), T(), T()
        FT = [T(name="ft%d" % i) for i in range(6)]
        Ton, TrC, TrS, Tposi, Tang, Tangk, Tangi, Trl = [T() for _ in range(8)]
        CKV = [[[T() for _ in range(NT)] for _ in range(2)] for _ in range(L)]
        CKT = [[T() for _ in range(NT)] for _ in range(L)]
        KPE = [[T() for _ in range(NT)] for _ in range(L)]
        ST32 = [[T() for _ in range(HH)] for _ in range(L)]
        STB = [[T() for _ in range(HH)] for _ in range(L)]
        DC = [T() for _ in range(HH)]
        Tlb, Tlbl, Tlbe, Tlbm, Tlbt, Tlsel, Tsmall = [T() for _ in range(7)]
        OT = [T(name="o%d" % i) for i in range(16)]
        VT = [T(name="v%d" % i) for i in range(4)]
        Tqin, Tkin, Tkout, Tatm, Tqn, Tqpe = [T() for _ in range(6)]
        KEX = [T() for _ in range(4)]
        CQ = [T() for _ in range(4)]
        QP = [T(), T()]
        PTT = [T(), T()]
        AC = [T(), T()]
        U5 = [T() for _ in range(4)]
        vtok = scr[:, 16:24, :].rearrange("p (tb a) t -> p tb (a t)", tb=4)
        qin, kin, kout = scr[:, 24, :], scr[:, 25, :], scr[:, 26, :]
        kexp = scr[:, 27:31, :].rearrange("p tb (c d) -> p tb c d", c=4)
        atm = scr[:, 31, :]
        cq = scr[:, 32:36, :]
        qn, qpe = scr[:, 36, :], scr[:, 37, :]
        qp = scr[:, 38:40, :]
        pT = scr[:, 40:42, :]
        ac = scr[:, 42:44, :]
        ring_cnt = [0]

        def ring_load(src, ncols=SLOTW):
            s = ring_cnt[0] % NSLOT
            ring_cnt[0] += 1
            S.dma("pool", ring[:, s, :ncols], src[:, :ncols], writes=[RT[s]], sem="r%d" % s)
            return s

        S.dma("sp", cst[:], cstd, writes=[Tc])
        S.dma("sp", c32[:], cT, writes=[Tc32])
        S.dma("sp", ng_sb[:], ngd.rearrange("l p c -> p l c"), writes=[Tng])
        S.dma("sp", bada_sb[:], bada.rearrange("l p c -> p l c"), writes=[Tbada])
        S.dma("sp", fng_sb[:], fng, writes=[Tfng])
        S.op("dve", lambda e: e.tensor_copy(cb16[:, 0:768], cst[:, 0:768]), reads=[Tc], writes=[Tcb])
        for i, v in enumerate((1.0 / 2048, 1.0 / 512, 1.0 / 256, 1.0 / 128)):
            S.op("dve", lambda e, i=i, v=v: e.memset(onesd[:, i, :], v), writes=[Tones])
        S.op("dve", lambda e: e.memset(ones1[:], 1.0), writes=[Tones])
        ident = cb16[:, 0:128]
        eps_ap = cst[:, C_EPS:C_EPS + 1]

        S.dma("sp", lbl[:], lbd, writes=[Tlbl])
        S.dma("sp", lsel_sb[:], lsel, writes=[Tlsel])
        S.dma("sp", small_sb[:], smalld.rearrange("l p c -> p l c"), writes=[Tsmall])
        S.op("dve", lambda e: e.tensor_reduce(lbm[:], lbl[:], mybir.AxisListType.X, ALU.max), reads=[Tlbl], writes=[Tlbm])
        S.op("dve", lambda e: e.tensor_tensor(lbe[:], lbl[:], lbm[:].unsqueeze(2).to_broadcast([128, HH, DEPTH]), ALU.subtract),
             reads=[Tlbl, Tlbm], writes=[Tlbe])
        S.op("act", lambda e: e.activation(lbe[:], lbe[:], AF.Exp), reads=[Tlbe], writes=[Tlbe])
        S.op("dve", lambda e: e.tensor_reduce(lbm[:], lbe[:], mybir.AxisListType.X, ALU.add), reads=[Tlbe], writes=[Tlbm])
        S.op("dve", lambda e: e.reciprocal(lbm[:], lbm[:]), reads=[Tlbm], writes=[Tlbm])
        for li in range(L):
            S.op("dve", lambda e, li=li: e.tensor_tensor(
                lbt[:], lbe[:], lsel_sb[:, li * DEPTH:(li + 1) * DEPTH].unsqueeze(1).to_broadcast([128, HH, DEPTH]), ALU.mult),
                reads=[Tlbe, Tlsel], writes=[Tlbt])
            S.op("dve", lambda e, li=li: e.tensor_reduce(lbv[:, li, 0, :], lbt[:], mybir.AxisListType.X, ALU.add),
                 reads=[Tlbt], writes=[Tlb])
            S.op("dve", lambda e, li=li: e.tensor_tensor(lbv[:, li, 0, :], lbv[:, li, 0, :], lbm[:], ALU.mult),
                 reads=[Tlb, Tlbm], writes=[Tlb])
            S.op("dve", lambda e, li=li: e.tensor_scalar(lbv[:, li, 1, :], lbv[:, li, 0, :], -1.0, 1.0, ALU.mult, ALU.add),
                 reads=[Tlb], writes=[Tlb])
            S.op("dve", lambda e, li=li: e.tensor_scalar(lbv[:, li, 2, :], lbv[:, li, 1, :], -1.0, None, ALU.mult),
                 reads=[Tlb], writes=[Tlb])

        S.op("act", lambda e: e.activation(cact[:], c32[:], AF.Silu), reads=[Tc32], writes=[Tcact])
        for li in range(L):
            mps = P[7]
            for ci in range(N_ADA_T):
                s = ring_load(wada[li, ci])
                for k in range(KC):
                    S.op("pe", lambda e, s=s, k=k, ci=ci: e.matmul(
                        mps[:, ci * NSEQ:(ci + 1) * NSEQ], ring[:, s, k * 128:(k + 1) * 128], cact[:, k, :],
                        start=(k == 0), stop=(k == KC - 1)),
                        reads=[RT[s], Tcact], writes=[PT[7]])
            mv = modv[:, li].rearrange("p a b m s -> p (a b m) s")
            for b in range(NSEQ):
                S.op("dve", lambda e, b=b, li=li, mv=mv: e.tensor_tensor(
                    mv[:, :, b], mps[:, 0:N_ADA_T * NSEQ].rearrange("p (c s) -> p c s", s=NSEQ)[:, :, b],
                    bada_sb[:, li, :], ALU.add), reads=[PT[7], Tbada], writes=[Tmod])
            for sub in range(3):
                for b in range(NSEQ):
                    S.op("dve", lambda e, sub=sub, b=b, li=li: e.scalar_tensor_tensor(
                        modv[:, li, sub, 1, :, b], modv[:, li, sub, 1, :, b], 1.0, ng_sb[:, li, sub * 16:(sub + 1) * 16],
                        ALU.add, ALU.mult), reads=[Tmod, Tng], writes=[Tmod])
                if sub != 1:
                    S.op("dve", lambda e, sub=sub, li=li: e.tensor_scalar(
                        modv[:, li, sub, 2], modv[:, li, sub, 2], 0.5, None, ALU.mult), reads=[Tmod], writes=[Tmod])

        def rstd_from(ps_idx, eps_col=eps_ap):
            S.op("act", lambda e: e.activation(rstd[:], P[ps_idx][:], AF.Ln, bias=eps_col, scale=1.0),
                 reads=[PT[ps_idx], Tc], writes=[Trstd])
            S.op("act", lambda e: e.activation(rstd[:], rstd[:], AF.Exp, scale=-0.5), reads=[Trstd], writes=[Trstd])

        def norm_mod(li, sub, b):
            for k in range(KC):
                S.op("act", lambda e, k=k: e.activation(sq[:, k % 2, :], x_sb[:, k, :], AF.Square),
                     reads=[X[k]], writes=[Tsq[k % 2]])
                S.op("pe", lambda e, k=k: e.matmul(P[6][:], onesd[:, 0, :], sq[:, k % 2, :], start=(k == 0), stop=(k == KC - 1)),
                     reads=[Tsq[k % 2], Tones], writes=[PT[6]])
            rstd_from(6)
            for k in range(KC):
                S.op("dve", lambda e, k=k: e.scalar_tensor_tensor(
                    tmpa[:, k % 2, :], x_sb[:, k, :], modv[:, li, sub, 1, k, b:b + 1], rstd[:], ALU.mult, ALU.mult),
                    reads=[X[k], Tmod, Trstd], writes=[Ttmp[k % 2]])
                S.op("act", lambda e, k=k: e.activation(h_sb[:, k, :], tmpa[:, k % 2, :], AF.Identity,
                                                        bias=modv[:, li, sub, 0, k, b:b + 1], scale=1.0),
                     reads=[Ttmp[k % 2], Tmod], writes=[H[k]])

        def ffn(li, which, b):
            sub = 0 if which == 0 else 2
            base = 0 if which == 0 else N_FFN_T + N_MIX_T
            norm_mod(li, sub, b)
            ti = base
            for f in range(FC):
                sg_ = ring_load(wst[li, ti]); su_ = ring_load(wst[li, ti + 1]); ti += 2
                gp, up = (f % 2), 2 + (f % 2)
                for k in range(KC):
                    S.op("pe", lambda e, s=sg_, k=k, gp=gp: e.matmul(
                        P[gp][:], ring[:, s, k * 128:(k + 1) * 128], h_sb[:, k, :], start=(k == 0), stop=(k == KC - 1)),
                        reads=[RT[sg_], H[k]], writes=[PT[gp]])
                for k in range(KC):
                    S.op("pe", lambda e, s=su_, k=k, up=up: e.matmul(
                        P[up][:], ring[:, s, k * 128:(k + 1) * 128], h_sb[:, k, :], start=(k == 0), stop=(k == KC - 1)),
                        reads=[RT[su_], H[k]], writes=[PT[up]])
                S.op("act", lambda e, f=f, gp=gp: e.activation(sgt[:, f % 2, :], P[gp][:], AF.Silu),
                     reads=[PT[gp]], writes=[Tsg[f % 2]])
                S.op("dve", lambda e, f=f, up=up: e.tensor_tensor(scr[:, f, :], sgt[:, f % 2, :], P[up][:], ALU.mult),
                     reads=[Tsg[f % 2], PT[up]], writes=[HID[f]])
            for m in range(KC):
                yp = 4 + (m % 2)
                for q in range(4):
                    s = ring_load(wst[li, ti], 11 * 128); ti += 1
                    for f2 in range(11):
                        f = q * 11 + f2
                        S.op("pe", lambda e, s=s, f2=f2, f=f, yp=yp: e.matmul(
                            P[yp][:], ring[:, s, f2 * 128:(f2 + 1) * 128], scr[:, f, :], start=(f == 0), stop=(f == FC - 1)),
                            reads=[RT[s], HID[f]], writes=[PT[yp]])
                S.op("dve", lambda e, m=m, yp=yp: e.scalar_tensor_tensor(
                    x_sb[:, m, :], P[yp][:], modv[:, li, sub, 2, m, b:b + 1], x_sb[:, m, :], ALU.mult, ALU.add),
                    reads=[PT[yp], Tmod, X[m]], writes=[X[m]])

        def rope_tables(b, j):
            S.dma("sp", posi[0:64, :], posd[b:b + 1, j * TT:(j + 1) * TT].partition_broadcast(64), writes=[Tposi])
            S.op("dve", lambda e: e.tensor_copy(ang[0:64, :], posi[0:64, :]), reads=[Tposi], writes=[Tang])
            S.op("dve", lambda e: e.tensor_scalar(ang[0:64, :], ang[0:64, :], cst[0:64, C_INV:C_INV + 1], None, ALU.mult),
                 reads=[Tang, Tc], writes=[Tang])
            for (dst, Td, phase, sc_col, bi_col) in ((ropeC, TrC, math.pi / 2, C_ONE, C_NPI), (ropeS, TrS, 0.0, C_SGN, C_SGNB)):
                d = dst[0:64, :]
                kf_ = angk[0:64, :]
                ki_ = angi[0:64, :]
                S.op("dve", lambda e, d=d, phase=phase: e.tensor_scalar(d, ang[0:64, :], phase + math.pi, None, ALU.add),
                     reads=[Tang], writes=[Td])
                S.op("dve", lambda e, d=d: e.tensor_scalar(kf_, d, 1.0 / (2 * math.pi), None, ALU.mult), reads=[Td], writes=[Tangk])
                S.op("dve", lambda e: e.tensor_copy(ki_, kf_), reads=[Tangk], writes=[Tangi])
                S.op("dve", lambda e: e.tensor_copy(kf_, ki_), reads=[Tangi], writes=[Tangk])
                S.op("dve", lambda e, d=d: e.scalar_tensor_tensor(d, kf_, -6.28125, d, ALU.mult, ALU.add), reads=[Tangk, Td], writes=[Td])
                S.op("dve", lambda e, d=d: e.scalar_tensor_tensor(d, kf_, -0.0019353071795864769, d, ALU.mult, ALU.add),
                     reads=[Tangk, Td], writes=[Td])
                S.op("dve", lambda e, d=d: e.tensor_scalar(kf_, d, 0.0, 2 * math.pi, ALU.is_lt, ALU.mult), reads=[Td], writes=[Tangk])
                S.op("dve", lambda e, d=d: e.tensor_tensor(d, d, kf_, ALU.add), reads=[Tangk, Td], writes=[Td])
                S.op("dve", lambda e, d=d: e.tensor_scalar(kf_, d, 2 * math.pi, -2 * math.pi, ALU.is_ge, ALU.mult), reads=[Td], writes=[Tangk])
                S.op("dve", lambda e, d=d: e.tensor_tensor(d, d, kf_, ALU.add), reads=[Tangk, Td], writes=[Td])
                S.op("act", lambda e, d=d, sc_col=sc_col, bi_col=bi_col: e.activation(
                    d, d, AF.Sin, bias=cst[0:64, bi_col:bi_col + 1], scale=cst[0:64, sc_col:sc_col + 1]),
                    reads=[Td, Tc], writes=[Td])

        def mm_chunk(ps_idx, slot, rows=128, col0=0, ncol=128, rhs_tiles=None):
            for k in range(KC):
                S.op("pe", lambda e, k=k: e.matmul(
                    P[ps_idx][0:rows, :], ring[:, slot, k * 128 + col0:k * 128 + col0 + ncol], h_sb[:, k, :],
                    start=(k == 0), stop=(k == KC - 1)), reads=[RT[slot], H[k]], writes=[PT[ps_idx]])

        def mixer(li, b, j):
            ti = N_FFN_T
            sub = 1
            tcols = slice(j * TT, (j + 1) * TT)
            norm_mod(li, sub, b)
            for cb in range(2):
                for kq in range(4):
                    s_ = ring_load(wst[li, ti]); ti += 1
                    for k2 in range(4):
                        k = kq * 4 + k2
                        for tb in range(4):
                            S.op("pe", lambda e, s_=s_, k=k, k2=k2, tb=tb: e.matmul(
                                P[tb][:], h_sb[:, k, tb * 128:(tb + 1) * 128], ring[:, s_, k2 * 512:(k2 + 1) * 512],
                                start=(k == 0), stop=(k == KC - 1)), reads=[RT[s_], H[k]], writes=[PT[tb]])
                for tb in range(4):
                    S.op("act", lambda e, tb=tb, cb=cb: e.activation(vtok[:, tb, cb * 512:(cb + 1) * 512], P[tb][:], AF.Copy),
                         reads=[PT[tb]], writes=[VT[tb]])
            for c in range(4):
                s_ = ring_load(wst[li, ti]); ti += 1
                mm_chunk(c, s_)
                S.op("act", lambda e, c=c: e.activation(ft[:, c, :], P[c][:], AF.Copy), reads=[PT[c]], writes=[FT[c]])
                S.op("act", lambda e, c=c: e.activation(sq[:, c % 2, :], ft[:, c, :], AF.Square), reads=[FT[c]], writes=[Tsq[c % 2]])
                S.op("pe", lambda e, c=c: e.matmul(P[6][:], onesd[:, 1, :], sq[:, c % 2, :], start=(c == 0), stop=(c == 3)),
                     reads=[Tsq[c % 2], Tones], writes=[PT[6]])
            rstd_from(6)
            for c in range(4):
                S.op("dve", lambda e, c=c: e.scalar_tensor_tensor(
                    cq[:, c, :], ft[:, c, :], small_sb[:, li, c:c + 1], rstd[:], ALU.mult, ALU.mult),
                    reads=[FT[c], Tsmall, Trstd], writes=[CQ[c]])
            for c in range(2):
                s_ = ring_load(wst[li, ti]); ti += 1
                mm_chunk(4 + c, s_)
                S.op("act", lambda e, c=c: e.activation(ft[:, 4 + c, :], P[4 + c][:], AF.Copy), reads=[PT[4 + c]], writes=[FT[4 + c]])
                S.op("act", lambda e, c=c: e.activation(sq[:, c % 2, :], ft[:, 4 + c, :], AF.Square), reads=[FT[4 + c]], writes=[Tsq[c % 2]])
                S.op("pe", lambda e, c=c: e.matmul(P[6][:], onesd[:, 2, :], sq[:, c % 2, :], start=(c == 0), stop=(c == 1)),
                     reads=[Tsq[c % 2], Tones], writes=[PT[6]])
            rstd_from(6)
            for c in range(2):
                S.op("dve", lambda e, c=c: e.scalar_tensor_tensor(
                    ckvT[:, li, c, tcols], ft[:, 4 + c, :], small_sb[:, li, 4 + c:5 + c], rstd[:], ALU.mult, ALU.mult),
                    reads=[FT[4 + c], Tsmall, Trstd], writes=[CKV[li][c][j]])
            for c in range(2):
                for tb in range(4):
                    S.op("pe", lambda e, c=c, tb=tb: e.transpose(
                        P7b[:, (c * 4 + tb) * 128:(c * 4 + tb + 1) * 128],
                        ckvT[:, li, c, j * TT + tb * 128: j * TT + (tb + 1) * 128], ident),
                        reads=[CKV[li][c][j], Tcb], writes=[PT[7]])
            for c in range(2):
                S.op("act", lambda e, c=c: e.activation(
                    ckvtok[:, li, j * 4:(j + 1) * 4, c * 128:(c + 1) * 128],
                    P7b[:, c * 512:(c + 1) * 512].rearrange("p (tb d) -> p tb d", tb=4), AF.Copy),
                    reads=[PT[7]], writes=[CKT[li][j]])
            s_ = ring_load(wst[li, ti]); ti += 1
            mm_chunk(0, s_, rows=64, col0=0, ncol=64)
            mm_chunk(1, s_, rows=64, col0=64, ncol=64)
            S.op("dve", lambda e: e.tensor_tensor(tmpa[0:64, 0, :], P[0][0:64, :], ropeC[0:64, :], ALU.mult),
                 reads=[PT[0], TrC], writes=[Ttmp[0]])
            S.op("dve", lambda e: e.tensor_tensor(tmpa[0:64, 1, :], P[1][0:64, :], ropeS[0:64, :], ALU.mult),
                 reads=[PT[1], TrS], writes=[Ttmp[1]])
            S.op("dve", lambda e: e.tensor_tensor(kpeT[0:64, li, tcols], tmpa[0:64, 0, :], tmpa[0:64, 1, :], ALU.add),
                 reads=[Ttmp[0], Ttmp[1]], writes=[KPE[li][j]])
            s_uk = ring_load(wst[li, ti]); ti += 1
            s_uv = ring_load(wst[li, ti]); ti += 1
            nkb = 4 * (j + 1)
            s_wq = None
            for hd in range(HH):
                if hd % 2 == 0:
                    s_wq = ring_load(wst[li, ti]); ti += 1
                off = (hd % 2) * 256
                for (pi_, c0, nco, rows) in ((0, off, 128, 128), (1, off + 128, 64, 64), (2, off + 192, 64, 64)):
                    for k in range(4):
                        S.op("pe", lambda e, k=k, pi_=pi_, c0=c0, nco=nco, rows=rows, s_wq=s_wq: e.matmul(
                            P[pi_][0:rows, :], ring[:, s_wq, k * 512 + c0:k * 512 + c0 + nco], cq[:, k, :],
                            start=(k == 0), stop=(k == 3)), reads=[RT[s_wq], CQ[k]], writes=[PT[pi_]])
                S.op("act", lambda e: e.activation(qn, P[0][:], AF.Copy), reads=[PT[0]], writes=[Tqn])
                S.op("dve", lambda e: e.tensor_tensor(tmpa[0:64, 0, :], P[1][0:64, :], ropeC[0:64, :], ALU.mult),
                     reads=[PT[1], TrC], writes=[Ttmp[0]])
                S.op("dve", lambda e: e.tensor_tensor(tmpa[0:64, 1, :], P[2][0:64, :], ropeS[0:64, :], ALU.mult),
                     reads=[PT[2], TrS], writes=[Ttmp[1]])
                S.op("dve", lambda e: e.tensor_tensor(qpe[0:64, :], tmpa[0:64, 0, :], tmpa[0:64, 1, :], ALU.add),
                     reads=[Ttmp[0], Ttmp[1]], writes=[Tqpe])
                for cc in range(2):
                    S.op("pe", lambda e, cc=cc, hd=hd: e.matmul(
                        P[3 + cc][:], ring[:, s_uk, hd * 256 + cc * 128:hd * 256 + (cc + 1) * 128], qn, start=True, stop=True),
                        reads=[RT[s_uk], Tqn], writes=[PT[3 + cc]])
                    S.op("act", lambda e, cc=cc: e.activation(qp[:, cc, :], P[3 + cc][:], AF.Copy), reads=[PT[3 + cc]], writes=[QP[cc]])
                for kb in range(nkb):
                    kbi = kb - 4 * j
                    q0 = max(0, kbi) * 128
                    sp_ = 3 + (kb % 2)
                    pj = kb // 4
                    pb = kb % 2
                    kc = slice(kb * 128, (kb + 1) * 128)
                    S.op("pe", lambda e, sp_=sp_, kc=kc, q0=q0: e.matmul(
                        P[sp_][:, q0:], ckvT[:, li, 0, kc], qp[:, 0, q0:], start=True, stop=False),
                        reads=[CKV[li][0][pj], QP[0]], writes=[PT[sp_]])
                    S.op("pe", lambda e, sp_=sp_, kc=kc, q0=q0: e.matmul(
                        P[sp_][:, q0:], ckvT[:, li, 1, kc], qp[:, 1, q0:], start=False, stop=False),
                        reads=[CKV[li][1][pj], QP[1]], writes=[PT[sp_]])
                    S.op("pe", lambda e, sp_=sp_, kc=kc, q0=q0: e.matmul(
                        P[sp_][:, q0:], kpeT[0:64, li, kc], qpe[0:64, q0:], start=False, stop=True),
                        reads=[KPE[li][pj], Tqpe], writes=[PT[sp_]])
                    S.op("act", lambda e, sp_=sp_, pb=pb, q0=q0: e.activation(pT[:, pb, q0:], P[sp_][:, q0:], AF.Exp, scale=SCALE),
                         reads=[PT[sp_]], writes=[PTT[pb]])
                    if kbi >= 0:
                        S.op("dve", lambda e, pb=pb, q0=q0: e.tensor_tensor(
                            pT[:, pb, q0:q0 + 128], pT[:, pb, q0:q0 + 128], cb16[:, C_TRI:C_TRI + 128], ALU.mult),
                            reads=[PTT[pb], Tcb], writes=[PTT[pb]])
                    S.op("pe", lambda e, pb=pb, q0=q0, kb=kb: e.matmul(
                        P[5][:, q0:], ones1[:], pT[:, pb, q0:], start=(kb == 0), stop=(kb == nkb - 1)),
                        reads=[PTT[pb], Tones], writes=[PT[5]])
                    for cc in range(2):
                        S.op("pe", lambda e, pb=pb, q0=q0, kb=kb, cc=cc: e.matmul(
                            P[6 + cc][:, q0:], ckvtok[:, li, kb, cc * 128:(cc + 1) * 128], pT[:, pb, q0:],
                            start=(kb == 0), stop=(kb == nkb - 1)),
                            reads=[PTT[pb], CKT[li][pj]], writes=[PT[6 + cc]])
                S.op("dve", lambda e: e.reciprocal(rl[:], P[5][:]), reads=[PT[5]], writes=[Trl])
                for cc in range(2):
                    S.op("act", lambda e, cc=cc: e.activation(ac[:, cc, :], P[6 + cc][:], AF.Copy), reads=[PT[6 + cc]], writes=[AC[cc]])
                for cc in range(2):
                    S.op("pe", lambda e, cc=cc, hd=hd: e.matmul(
                        P[0][:], ring[:, s_uv, cc * 1024 + hd * 128:cc * 1024 + (hd + 1) * 128], ac[:, cc, :],
                        start=(cc == 0), stop=(cc == 1)), reads=[RT[s_uv], AC[cc]], writes=[PT[0]])
                S.op("dve", lambda e, hd=hd: e.tensor_tensor(scr[:, 8 + hd, :], P[0][:], rl[:], ALU.mult),
                     reads=[PT[0], Trl], writes=[OT[8 + hd]])
            if j == 0:
                S.op("dve", lambda e: e.memset(S32[:, li], 0.0), writes=ST32[li])
                S.op("dve", lambda e: e.memset(Sbf[:, li], 0.0), writes=STB[li])
            for hd in range(HH):
                s_q = ring_load(wst[li, ti]); s_f = ring_load(wst[li, ti + 1]); s_g = ring_load(wst[li, ti + 2]); ti += 3
                mm_chunk(0, s_q); mm_chunk(1, s_f); mm_chunk(2, s_g)
                lb_ap = lbv[:, li, 0, hd:hd + 1]; oml_ap = lbv[:, li, 1, hd:hd + 1]; noml_ap = lbv[:, li, 2, hd:hd + 1]
                qf, ef, lf, kf, bt, gs = [ft[:, i, :] for i in range(6)]
                S.op("act", lambda e: e.activation(qf, P[0][:], AF.Silu), reads=[PT[0]], writes=[FT[0]])
                S.op("act", lambda e: e.activation(gs, P[2][:], AF.Silu), reads=[PT[2]], writes=[FT[5]])
                S.op("act", lambda e: e.activation(ef, P[1][:], AF.Exp, scale=-1.0), reads=[PT[1]], writes=[FT[1]])
                S.op("dve", lambda e: e.tensor_scalar(ef, ef, 1.0, None, ALU.add), reads=[FT[1]], writes=[FT[1]])
                S.op("dve", lambda e: e.reciprocal(ef, ef), reads=[FT[1]], writes=[FT[1]])
                S.op("act", lambda e, oml_ap=oml_ap, lb_ap=lb_ap: e.activation(lf, ef, AF.Ln, bias=lb_ap, scale=oml_ap),
                     reads=[FT[1], Tlb], writes=[FT[2]])
                S.op("dve", lambda e, oml_ap=oml_ap, noml_ap=noml_ap: e.tensor_scalar(kf, ef, noml_ap, oml_ap, ALU.mult, ALU.add),
                     reads=[FT[1], Tlb], writes=[FT[3]])
                S.op("dve", lambda e: e.tensor_tensor_scan(bt, cst[:, C_SM:C_SM + TT], lf, 0.0, ALU.mult, ALU.add),
                     reads=[FT[2], Tc], writes=[FT[4]])
                S.op("act", lambda e: e.activation(ef, bt, AF.Exp), reads=[FT[4]], writes=[FT[1]])
                S.op("act", lambda e: e.activation(lf, bt, AF.Exp, scale=-1.0), reads=[FT[4]], writes=[FT[2]])
                S.op("act", lambda e, hd=hd: e.activation(dcs[:, hd, :], ft[:, 4, 31::32], AF.Exp), reads=[FT[4]], writes=[DC[hd]])
                S.op("dve", lambda e: e.tensor_tensor(qin, qf, ef, ALU.mult), reads=[FT[0], FT[1]], writes=[Tqin])
                S.op("dve", lambda e: e.tensor_tensor(kf, kf, lf, ALU.mult), reads=[FT[3], FT[2]], writes=[FT[3]])
                S.op("act", lambda e: e.activation(kin, kf, AF.Copy), reads=[FT[3]], writes=[Tkin])
                S.op("dve", lambda e, hd=hd: e.tensor_tensor(
                    kout.rearrange("p (c t) -> p c t", t=32), ft[:, 3, :].rearrange("p (c t) -> p c t", t=32),
                    dcs[:, hd, :].unsqueeze(2).to_broadcast([128, 16, 32]), ALU.mult),
                    reads=[FT[3], DC[hd]], writes=[Tkout])
                for tb in range(4):
                    S.op("pe", lambda e, tb=tb: e.transpose(P7b[:, tb * 128:(tb + 1) * 128], kout[:, tb * 128:(tb + 1) * 128], ident),
                         reads=[Tkout, Tcb], writes=[PT[7]])
                for c in range(4):
                    S.op("act", lambda e, c=c: e.activation(
                        kexp[:, :, c, :], P7b[:, 0:512].rearrange("p (tb d) -> p tb d", tb=4), AF.Identity,
                        scale=cst[:, C_CM + c:C_CM + c + 1]), reads=[PT[7], Tc], writes=KEX)
                for tb in range(4):
                    S.op("pe", lambda e, tb=tb: e.matmul(
                        P[3][:, tb * 128:(tb + 1) * 128], kin[:, tb * 128:(tb + 1) * 128], qin[:, tb * 128:(tb + 1) * 128],
                        start=True, stop=True), reads=[Tkin, Tqin], writes=[PT[3]])
                S.op("dve", lambda e: e.tensor_tensor(atm, P[3][:], cst[:, C_BD:C_BD + TT], ALU.mult), reads=[PT[3], Tc], writes=[Tatm])
                for tb in range(4):
                    S.op("pe", lambda e, tb=tb, hd=hd: e.matmul(
                        P[4][:, tb * 128:(tb + 1) * 128], vtok[:, tb, hd * 128:(hd + 1) * 128], atm[:, tb * 128:(tb + 1) * 128],
                        start=True, stop=False), reads=[VT[tb], Tatm], writes=[PT[4]])
                    for c in range(4):
                        ch = tb * 4 + c
                        S.op("pe", lambda e, ch=ch, hd=hd, c=c: e.matmul(
                            P[4][:, ch * 32:(ch + 1) * 32], Sbf[:, li, hd, :], qin[:, ch * 32:(ch + 1) * 32],
                            start=False, stop=(c == 3)), reads=[STB[li][hd], Tqin], writes=[PT[4]])
                        S.op("pe", lambda e, tb=tb, c=c, hd=hd: e.matmul(
                            P[5][:, c * 128:(c + 1) * 128], kexp[:, tb, c, :], vtok[:, tb, hd * 128:(hd + 1) * 128],
                            start=True, stop=True), reads=[KEX[tb], VT[tb]],
                            writes=([U5[c], PT[5]] if (hd == 0 and tb == 0) else [U5[c]]))
                        S.op("dve", lambda e, ch=ch, hd=hd, c=c: e.scalar_tensor_tensor(
                            S32[:, li, hd, :], S32[:, li, hd, :], dcs[:, hd, ch:ch + 1], P[5][:, c * 128:(c + 1) * 128],
                            ALU.mult, ALU.add), reads=[ST32[li][hd], DC[hd], U5[c]], writes=[ST32[li][hd]])
                        S.op("act", lambda e, hd=hd: e.activation(Sbf[:, li, hd, :], S32[:, li, hd, :], AF.Copy),
                             reads=[ST32[li][hd]], writes=[STB[li][hd]])
                S.op("act", lambda e: e.activation(sq[:, 0, :], P[4][:], AF.Square), reads=[PT[4]], writes=[Tsq[0]])
                S.op("pe", lambda e: e.matmul(P[6][:], onesd[:, 3, :], sq[:, 0, :], start=True, stop=True),
                     reads=[Tsq[0], Tones], writes=[PT[6]])
                rstd_from(6)
                S.op("dve", lambda e: e.scalar_tensor_tensor(on_t[:], P[4][:], small_sb[:, li, 6:7], rstd[:], ALU.mult, ALU.mult),
                     reads=[PT[4], Tsmall, Trstd], writes=[Ton])
                S.op("dve", lambda e, hd=hd: e.tensor_tensor(scr[:, hd, :], on_t[:], gs, ALU.mult),
                     reads=[Ton, FT[5]], writes=[OT[hd]])
            for m in range(KC):
                s_ = ring_load(wst[li, ti]); ti += 1
                yp = m % 2
                for k in range(KC):
                    S.op("pe", lambda e, s_=s_, k=k, yp=yp: e.matmul(
                        P[yp][:], ring[:, s_, k * 128:(k + 1) * 128], scr[:, k, :], start=(k == 0), stop=(k == KC - 1)),
                        reads=[RT[s_], OT[k]], writes=[PT[yp]])
                S.op("dve", lambda e, m=m, yp=yp: e.scalar_tensor_tensor(
                    x_sb[:, m, :], P[yp][:], modv[:, li, sub, 2, m, b:b + 1], x_sb[:, m, :], ALU.mult, ALU.add),
                    reads=[PT[yp], Tmod, X[m]], writes=[X[m]])
            assert ti == N_FFN_T + N_MIX_T

        def final_norm():
            for k in range(KC):
                S.op("act", lambda e, k=k: e.activation(sq[:, k % 2, :], x_sb[:, k, :], AF.Square),
                     reads=[X[k]], writes=[Tsq[k % 2]])
                S.op("pe", lambda e, k=k: e.matmul(P[6][:], onesd[:, 0, :], sq[:, k % 2, :], start=(k == 0), stop=(k == KC - 1)),
                     reads=[Tsq[k % 2], Tones], writes=[PT[6]])
            rstd_from(6)
            for k in range(KC):
                S.op("dve", lambda e, k=k: e.scalar_tensor_tensor(
                    x_sb[:, k, :], x_sb[:, k, :], fng_sb[:, k:k + 1], rstd[:], ALU.mult, ALU.mult),
                    reads=[X[k], Tfng, Trstd], writes=[X[k]])

        for b in range(nseq):
            for j in range(ntiles):
                for kq in range(4):
                    S.dma("sp", x_sb[:, kq * 4:(kq + 1) * 4, :],
                          xT[b, kq * 512:(kq + 1) * 512, j * TT:(j + 1) * TT].rearrange("(k p) t -> p k t", p=128),
                          writes=X[kq * 4:(kq + 1) * 4])
                rope_tables(b, j)
                for li in range(L):
                    if "ffn1" in stages:
                        ffn(li, 0, b)
                        S.barrier()
                    if "mix" in stages:
                        mixer(li, b, j)
                        S.barrier()
                    if "ffn2" in stages:
                        ffn(li, 1, b)
                        S.barrier()
                if do_final:
                    final_norm()
                for kq in range(4):
                    S.dma("sp", outT[b, kq * 512:(kq + 1) * 512, j * TT:(j + 1) * TT].rearrange("(k p) t -> p k t", p=128),
                          x_sb[:, kq * 4:(kq + 1) * 4, :], reads=X[kq * 4:(kq + 1) * 4])
        S.emit()
    return nc


def shared_inputs(inp, layers):
    L = len(layers)
    sh = {}
    sh["wst"] = np.stack([layer_stream(inp, l) for l in layers])
    sh["wada"] = np.stack([ada_stream(inp["w_ada"][l]) for l in layers])
    sh["bada"] = np.stack([np.ascontiguousarray(inp["b_ada"][l].reshape(N_ADA_T, 128).T) for l in layers])
    sh["ng"] = np.stack([np.ascontiguousarray(inp["norm_g"][l].reshape(48, 128).T) for l in layers])
    small = np.zeros((L, 128, 16), np.float32)
    for i, l in enumerate(layers):
        small[i, :, 0:4] = inp["qa_norm_g"][l].reshape(4, 128).T
        small[i, :, 4:6] = inp["kva_norm_g"][l].reshape(2, 128).T
        small[i, :, 6] = inp["hg_norm_g"][l]
    sh["small"] = small
    sh["lblog"] = np.ascontiguousarray(inp["hg_lb_logits"].reshape(DEPTH, HH, 128).transpose(2, 1, 0))
    lsel = np.zeros((128, L * DEPTH), np.float32)
    for i, l in enumerate(layers):
        lsel[:, i * DEPTH + 1: i * DEPTH + l + 1] = 1.0
    sh["lsel"] = lsel
    sh["fng"] = np.ascontiguousarray(inp["final_norm_g"].reshape(KC, 128).T)
    sh["cst"] = make_consts()
    return sh


def core_inputs(xT_all, inp, core, sh):
    b0 = core * NSEQ
    m = dict(sh)
    m["xT"] = xT_all[b0:b0 + NSEQ]
    m["cT"] = np.ascontiguousarray(inp["c"][b0:b0 + NSEQ].reshape(NSEQ, KC, 128).transpose(2, 1, 0))
    m["pos"] = np.ascontiguousarray(inp["positions"][b0:b0 + NSEQ]).astype(np.int32)
    return m


_PROG_CACHE = {}


def run_layers(xT_all, inp, layers, do_final, cores=None, **bk):
    key = (len(layers), do_final, tuple(sorted(bk.items())))
    if key not in _PROG_CACHE:
        _PROG_CACHE[key] = build_program(len(layers), do_final, **bk)
    nc = _PROG_CACHE[key]
    sh = shared_inputs(inp, layers)
    cores = list(range(NCORES)) if cores is None else cores
    in_maps = [core_inputs(xT_all, inp, c, sh) for c in cores]
    res = run_bass_kernel_spmd(nc, in_maps, core_ids=list(range(len(cores))))
    return np.concatenate([r["outT"] for r in res.results], axis=0)


def kernel(**inputs):
    inp = {k: np.asarray(v) for k, v in inputs.items()}
    xT = np.ascontiguousarray(inp["x"].astype(np.float32).transpose(0, 2, 1))
    for l in range(DEPTH):
        xT = run_layers(xT, inp, [l], do_final=(l == DEPTH - 1))
    return np.ascontiguousarray(xT.transpose(0, 2, 1)).astype(np.float32)
```

```python
import contextlib
import math
import numpy as np
import concourse.bass as bass
import concourse.mybir as mybir
from concourse.bass_utils import run_bass_kernel_spmd

F32 = mybir.dt.float32
BF16 = mybir.dt.bfloat16
I32 = mybir.dt.int32
AF = mybir.ActivationFunctionType
ALU = mybir.AluOpType

NCORES = 8
D = 2048
SEQ = 2048
DEPTH = 4
NSEQ = 2
TT = 512
NT = SEQ // TT
KC = D // 128
DFF = 5632
FC = DFF // 128
HH = 8
D_IN = 4928
EPS = 1e-6
NSLOT = 8
BARRIER = False
SLOTW = 2048
SCALE = (128 + 64) ** -0.5

N_FFN_T = 2 * FC + 16 * 4
N_MIX_T = 8 + 7 + 4 + 1 + 1 + 24 + 16
N_LAYER_T = 2 * N_FFN_T + N_MIX_T
N_ADA_T = 144


class T:
    __slots__ = ("ap", "w", "r", "name")

    def __init__(self, ap=None, name=""):
        self.ap = ap
        self.w = None
        self.r = []
        self.name = name


class Sched:
    CE = ("pe", "act", "dve", "pool")

    def __init__(self, nc, n_dma_sems=12):
        self.nc = nc
        self.ops = {e: [] for e in ("pe", "act", "dve", "pool", "sp")}
        self.seen = {e: {} for e in self.ops}
        self.dma_cnt = {}
        self.n_misc = n_dma_sems
        self.misc_rr = 0
        self.pending = {e: [] for e in self.ops}

    def _need(self, e, tok):
        if tok[0] == "e" and tok[1] == e and e == "pe":
            return False
        return self.seen[e].get(tok[1], -1) < tok[2]

    def _mark(self, e, tok):
        if self.seen[e].get(tok[1], -1) < tok[2]:
            self.seen[e][tok[1]] = tok[2]
        if tok[0] == "e":
            self.ops[tok[1]][tok[2]][2] = True

    def op(self, e, fn, reads=(), writes=(), dma_sem=None):
        deps = {}

        def add(tok):
            if tok is None:
                return
            k = tok[1]
            if k not in deps or deps[k][2] < tok[2]:
                deps[k] = tok
        for t in reads:
            add(t.w)
        for t in writes:
            add(t.w)
            for rt in t.r:
                add(rt)
        for tok in self.pending[e]:
            add(tok)
        self.pending[e] = []
        if dma_sem is not None:
            n = self.dma_cnt.get(dma_sem, 0)
            if n > 0:
                add(("d", dma_sem, 16 * n))
        waits = []
        for tok in deps.values():
            if self._need(e, tok):
                waits.append(tok)
                self._mark(e, tok)
        idx = len(self.ops[e])
        self.ops[e].append([fn, waits, False, dma_sem])
        if dma_sem is not None:
            n = self.dma_cnt.get(dma_sem, 0) + 1
            self.dma_cnt[dma_sem] = n
            mytok = ("d", dma_sem, 16 * n)
        else:
            mytok = ("e", e, idx)
        for t in reads:
            t.r.append(mytok)
            if len(t.r) > 48:
                d = {}
                for rt in t.r:
                    if rt[1] not in d or d[rt[1]][2] < rt[2]:
                        d[rt[1]] = rt
                t.r = list(d.values())
        for t in writes:
            t.w = mytok
            t.r = []
        return mytok

    def misc_sem(self):
        s = "m%d" % self.misc_rr
        self.misc_rr = (self.misc_rr + 1) % self.n_misc
        return s

    def dma(self, q, out_ap, in_ap, reads=(), writes=(), sem=None):
        if sem is None:
            sem = self.misc_sem()
        return self.op(q, lambda e, o=out_ap, i=in_ap: e.dma_start(out=o, in_=i),
                       reads=reads, writes=writes, dma_sem=sem)

    def barrier(self):
        toks = []
        for f in self.CE:
            for i in range(len(self.ops[f]) - 1, -1, -1):
                if self.ops[f][i][3] is None:
                    toks.append(("e", f, i))
                    break
        for s, n in self.dma_cnt.items():
            toks.append(("d", s, 16 * n))
        for e in self.ops:
            self.pending[e] = list(self.pending[e]) + toks

    def emit(self, final_waits_engine="sp"):
        nc = self.nc
        fin = [("d", s, 16 * n) for s, n in self.dma_cnt.items()]
        sem_names = list(self.CE) + sorted(self.dma_cnt.keys())
        with contextlib.ExitStack() as st:
            sems = {}
            for n in sem_names:
                sems[n] = st.enter_context(nc.semaphore("s_" + n))
            sigcnt = {}
            for e in self.CE:
                c = 0
                arr = []
                for o in self.ops[e]:
                    if o[2]:
                        c += 1
                    arr.append(c)
                sigcnt[e] = arr

            def resolve(tok):
                if tok[0] == "e":
                    return sems[tok[1]], sigcnt[tok[1]][tok[2]]
                return sems[tok[1]], tok[2]

            def replay(ename, eng):
                for fn, waits, sig, dsem in self.ops[ename]:
                    for tok in waits:
                        s, v = resolve(tok)
                        eng.wait_ge(s, v)
                    ins = fn(eng)
                    if dsem is not None:
                        ins.then_inc(sems[dsem], 16)
                    elif sig:
                        ins.then_inc(sems[ename], 1)
                if ename == final_waits_engine:
                    for tok in fin:
                        s, v = resolve(tok)
                        eng.wait_ge(s, v)

            block = st.enter_context(nc.Block())

            @block.tensor
            def _(e):
                replay("pe", e)

            @block.scalar
            def _(e):
                replay("act", e)

            @block.vector
            def _(e):
                replay("dve", e)

            @block.gpsimd
            def _(e):
                replay("pool", e)

            @block.sync
            def _(e):
                replay("sp", e)


def _chunk_tile(w, col0, ncols=128):
    K = w.shape[0]
    return w[:, col0:col0 + ncols].reshape(K // 128, 128, ncols).transpose(1, 0, 2)


def _pad_tile(t):
    t = np.ascontiguousarray(t).reshape(128, -1)
    out = np.zeros((128, SLOTW), np.float32)
    out[:, :t.shape[1]] = t
    return out


def ffn_tiles(wg, wu, wd):
    tiles = []
    for f in range(FC):
        tiles.append(_pad_tile(_chunk_tile(wg, f * 128)))
        tiles.append(_pad_tile(_chunk_tile(wu, f * 128)))
    wd4 = wd.reshape(4, 11, 128, D)
    for m in range(KC):
        for q in range(4):
            tiles.append(_pad_tile(wd4[q, :, :, m * 128:(m + 1) * 128].transpose(1, 0, 2)))
    return tiles


ROT = np.concatenate([np.arange(32, 64), np.arange(0, 32)])


def mixer_tiles(w_in, w_q_up, w_kv_up, w_out):
    tiles = []
    for cb in range(2):
        for kq in range(4):
            blk = w_in[kq * 512:(kq + 1) * 512, 2048 + cb * 512: 2048 + (cb + 1) * 512]
            tiles.append(_pad_tile(blk.reshape(4, 128, 512).transpose(1, 0, 2)))
    for c in range(4):
        tiles.append(_pad_tile(_chunk_tile(w_in, 4096 + c * 128)))
    for c in range(2):
        tiles.append(_pad_tile(_chunk_tile(w_in, 4608 + c * 128)))
    kpe = w_in[:, 4864:4928]
    both = np.concatenate([kpe, kpe[:, ROT]], axis=1)
    tiles.append(_pad_tile(_chunk_tile(both, 0)))
    wkv = w_kv_up.reshape(256, HH, 256)
    tiles.append(_pad_tile(wkv[:, :, :128].transpose(2, 1, 0)))
    wuv = wkv[:, :, 128:].reshape(2, 128, HH, 128).transpose(1, 0, 2, 3)
    tiles.append(_pad_tile(wuv))
    for hd in range(HH):
        if hd % 2 == 0:
            cols = []
            for h2 in (hd, hd + 1):
                base = h2 * 192
                cols.append(w_q_up[:, base:base + 128])
                rp = w_q_up[:, base + 128:base + 192]
                cols.append(rp)
                cols.append(rp[:, ROT])
            blk = np.concatenate(cols, axis=1)
            tiles.append(_pad_tile(blk.reshape(4, 128, 512).transpose(1, 0, 2)))
        tiles.append(_pad_tile(_chunk_tile(w_in, 0 + hd * 128)))
        tiles.append(_pad_tile(_chunk_tile(w_in, 1024 + hd * 128)))
        tiles.append(_pad_tile(_chunk_tile(w_in, 3072 + hd * 128)))
    for m in range(KC):
        tiles.append(_pad_tile(_chunk_tile(w_out, m * 128)))
    return tiles


def layer_stream(inp, l):
    tiles = ffn_tiles(inp["ffn_w_gate"][l, 0], inp["ffn_w_up"][l, 0], inp["ffn_w_down"][l, 0])
    tiles += mixer_tiles(inp["w_in"][l], inp["w_q_up"][l], inp["w_kv_up"][l], inp["w_out"][l])
    tiles += ffn_tiles(inp["ffn_w_gate"][l, 1], inp["ffn_w_up"][l, 1], inp["ffn_w_down"][l, 1])
    assert len(tiles) == N_LAYER_T
    return np.stack(tiles)


def ada_stream(w_ada_l):
    return np.ascontiguousarray(
        w_ada_l.reshape(KC, 128, N_ADA_T, 128).transpose(2, 1, 0, 3)).reshape(N_ADA_T, 128, SLOTW)


def make_consts():
    c = np.zeros((128, 2048), np.float32)
    p = np.arange(128)
    c[:, 0:128] = np.eye(128)
    c[:, 128:256] = (p[:, None] <= p[None, :])
    bd = (p[:, None] // 32 == p[None, :] // 32) & (p[:, None] <= p[None, :])
    c[:, 256:768] = np.tile(bd, (1, 4))
    sm = np.ones(512); sm[::32] = 0.0
    c[:, 768:1280] = sm[None, :]
    c[:, 1280:1284] = (p[:, None] // 32 == np.arange(4)[None, :])
    half = 32
    inv = (10000.0 ** (-np.arange(half, dtype=np.float32) / half)).astype(np.float32)
    c[:, 1284] = np.tile(inv, 4)
    sgn = np.where((p % 64) < 32, -1.0, 1.0)
    c[:, 1285] = sgn
    c[:, 1286] = -math.pi * sgn
    c[:, 1287] = EPS
    c[:, 1288] = -math.pi
    c[:, 1289] = 1.0
    return c


C_ID, C_TRI, C_BD, C_SM, C_CM, C_INV, C_SGN, C_SGNB, C_EPS, C_NPI, C_ONE = 0, 128, 256, 768, 1280, 1284, 1285, 1286, 1287, 1288, 1289


def build_program(n_layers, do_final, stages=("ffn1", "mix", "ffn2"), ntiles=NT, nseq=NSEQ):
    nc = bass.Bass("TRN2", target_bir_lowering=False)
    L = n_layers
    xT = nc.dram_tensor("xT", [NSEQ, D, SEQ], F32, kind="ExternalInput").ap()
    outT = nc.dram_tensor("outT", [NSEQ, D, SEQ], F32, kind="ExternalOutput").ap()
    cT = nc.dram_tensor("cT", [128, KC, NSEQ], F32, kind="ExternalInput").ap()
    posd = nc.dram_tensor("pos", [NSEQ, SEQ], I32, kind="ExternalInput").ap()
    wst = nc.dram_tensor("wst", [L, N_LAYER_T, 128, SLOTW], F32, kind="ExternalInput").ap()
    wada = nc.dram_tensor("wada", [L, N_ADA_T, 128, SLOTW], F32, kind="ExternalInput").ap()
    bada = nc.dram_tensor("bada", [L, 128, N_ADA_T], F32, kind="ExternalInput").ap()
    ngd = nc.dram_tensor("ng", [L, 128, 48], F32, kind="ExternalInput").ap()
    smalld = nc.dram_tensor("small", [L, 128, 16], F32, kind="ExternalInput").ap()
    lbd = nc.dram_tensor("lblog", [128, HH, DEPTH], F32, kind="ExternalInput").ap()
    lsel = nc.dram_tensor("lsel", [128, L * DEPTH], F32, kind="ExternalInput").ap()
    fng = nc.dram_tensor("fng", [128, KC], F32, kind="ExternalInput").ap()
    cstd = nc.dram_tensor("cst", [128, 2048], F32, kind="ExternalInput").ap()

    with contextlib.ExitStack() as st:
        def sb(name, shape, dt=F32):
            return st.enter_context(nc.sbuf_tensor(name, shape, dt))

        def pst(name, shape, dt=F32):
            return st.enter_context(nc.psum_tensor(name, shape, dt))

        S = Sched(nc)
        x_sb = sb("x_sb", [128, KC, TT])
        h_sb = sb("h_sb", [128, KC, TT], BF16)
        ring = sb("ring", [128, NSLOT, SLOTW], BF16)
        wkeep = sb("wkeep", [128, 2, SLOTW], BF16)
        scr = sb("scr", [128, FC, TT], BF16)
        cst = sb("cst_sb", [128, 2048])
        cb16 = sb("cb16", [128, 1280], BF16)
        onesd = sb("onesd", [128, 4, 128], BF16)
        ones1 = sb("ones1", [128, 128], BF16)
        sq = sb("sq", [128, 2, TT], BF16)
        rstd = sb("rstd", [128, TT])
        modv = sb("modv", [128, L, 3, 3, KC, NSEQ])
        cact = sb("cact", [128, KC, NSEQ], BF16)
        c32 = sb("c32", [128, KC, NSEQ])
        ng_sb = sb("ng_sb", [128, L, 48])
        bada_sb = sb("bada_sb", [128, L, N_ADA_T])
        fng_sb = sb("fng_sb", [128, KC])
        ft = sb("ft", [128, 6, TT])
        tmpa = ft[:, 2:4, :]
        sgt = ft[:, 0:2, :]
        on_t = sb("on_t", [128, TT])
        ropeC = sb("ropeC", [128, TT]); ropeS = sb("ropeS", [128, TT])
        ang = ft[:, 4, :]; angk = ft[:, 5, :]
        angi = ft[:, 0, :].bitcast(I32); posi = ft[:, 1, :].bitcast(I32)
        rl = on_t
        ckvT = sb("ckvT", [128, 2, SEQ], BF16)
        ckvtok = sb("ckvtok", [128, SEQ // 128, 256], BF16)
        kpeT = sb("kpeT", [128, SEQ], BF16)
        S32 = sb("S32", [128, HH, 128])
        Sbf = sb("Sbf", [128, HH, 128], BF16)
        ckv_d = nc.dram_tensor("ckv_d", [L, 2, 128, SEQ], BF16).ap()
        ckt_d = nc.dram_tensor("ckt_d", [L, SEQ // 128, 128, 256], BF16).ap()
        kpe_d = nc.dram_tensor("kpe_d", [L, 64, SEQ], BF16).ap()
        s_d = nc.dram_tensor("s_d", [L, 128, HH * 128], F32).ap()
        dcs = sb("dcs", [128, HH, 16])
        lbv = sb("lbv", [128, L, 3, HH])
        lbl = sb("lbl", [128, HH, DEPTH]); lbe = sb("lbe", [128, HH, DEPTH]); lbm = sb("lbm", [128, HH]); lbt = sb("lbt", [128, HH, DEPTH])
        lsel_sb = sb("lsel_sb", [128, L * DEPTH]); small_sb = sb("small_sb", [128, L, 16])
        P = [pst("ps%d" % i, [128, TT]) for i in range(8)]
        P7b = P[7][:].bitcast(BF16)
        PT = [T(p, "ps%d" % i) for i, p in enumerate(P)]

        X = [T(name="x%d" % k) for k in range(KC)]
        H = [T(name="h%d" % k) for k in range(KC)]
        RT = [T(name="ring%d" % s) for s in range(NSLOT)]
        Tc, Tcb, Tones, Tsq, Trstd = T(), T(), T(), [T(), T()], T()
        Tmod, Tcact, Tc32, Tng, Tbada, Tfng = T(), T(), T(), T(), T(), T()
        FT = [T(name="ft%d" % i) for i in range(6)]
        Ttmp = [FT[2], FT[3]]
        Tsg = [FT[0], FT[1]]
        Ton, TrC, TrS = [T() for _ in range(3)]
        Tang, Tangk, Tangi, Tposi, Trl = FT[4], FT[5], FT[0], FT[1], Ton
        CKV = [[T() for _ in range(NT)] for _ in range(2)]
        CKT = [T() for _ in range(NT)]
        KPE = [T() for _ in range(NT)]
        ST32 = [T() for _ in range(HH)]
        STB = [T() for _ in range(HH)]
        DCK = [[[T() for _ in range(4)] for _ in range(NT)] for _ in range(L)]
        DS = [T() for _ in range(L)]
        DC = [T() for _ in range(HH)]
        Tlb, Tlbl, Tlbe, Tlbm, Tlbt, Tlsel, Tsmall = [T() for _ in range(7)]
        SCR = [T(name="scr%d" % i) for i in range(FC)]
        VT = [T(name="v%d" % i) for i in range(4)]
        for tb_ in range(4):
            SCR[16 + 2 * tb_] = VT[tb_]
            SCR[17 + 2 * tb_] = VT[tb_]
        OT = SCR[0:16]
        Tqin, Tkin, Tkout = SCR[24], SCR[25], SCR[26]
        KEX = SCR[27:31]
        Tatm = SCR[31]
        CQ = SCR[32:36]
        Tqn, Tqpe = SCR[36], SCR[37]
        QP = SCR[38:40]
        PTT = SCR[40:42]
        AC = SCR[42:44]
        vtok = scr[:, 16:24, :].rearrange("p (tb a) t -> p tb (a t)", tb=4)
        qin, kin, kout = scr[:, 24, :], scr[:, 25, :], scr[:, 26, :]
        kexp = scr[:, 27:31, :].rearrange("p tb (c d) -> p tb c d", c=4)
        atm = scr[:, 31, :]
        cq = scr[:, 32:36, :]
        qn, qpe = scr[:, 36, :], scr[:, 37, :]
        qp = scr[:, 38:40, :]
        pT = scr[:, 40:42, :]
        ac = scr[:, 42:44, :]
        HID = SCR
        TWK = [T(), T()]
        ring_cnt = [0]

        def ring_load(src, ncols=SLOTW):
            s = ring_cnt[0] % NSLOT
            ring_cnt[0] += 1
            S.dma("pool", ring[:, s, :ncols], src[:, :ncols], writes=[RT[s]], sem="r%d" % s)
            return s

        S.dma("sp", cst[:], cstd, writes=[Tc])
        S.dma("sp", c32[:], cT, writes=[Tc32])
        S.dma("sp", ng_sb[:], ngd.rearrange("l p c -> p l c"), writes=[Tng])
        S.dma("sp", bada_sb[:], bada.rearrange("l p c -> p l c"), writes=[Tbada])
        S.dma("sp", fng_sb[:], fng, writes=[Tfng])
        S.op("dve", lambda e: e.tensor_copy(cb16[:, 0:768], cst[:, 0:768]), reads=[Tc], writes=[Tcb])
        for i, v in enumerate((1.0 / 2048, 1.0 / 512, 1.0 / 256, 1.0 / 128)):
            S.op("dve", lambda e, i=i, v=v: e.memset(onesd[:, i, :], v), writes=[Tones])
        S.op("dve", lambda e: e.memset(ones1[:], 1.0), writes=[Tones])
        ident = cb16[:, 0:128]
        eps_ap = cst[:, C_EPS:C_EPS + 1]

        S.dma("sp", lbl[:], lbd, writes=[Tlbl])
        S.dma("sp", lsel_sb[:], lsel, writes=[Tlsel])
        S.dma("sp", small_sb[:], smalld.rearrange("l p c -> p l c"), writes=[Tsmall])
        S.op("dve", lambda e: e.tensor_reduce(lbm[:], lbl[:], mybir.AxisListType.X, ALU.max), reads=[Tlbl], writes=[Tlbm])
        S.op("dve", lambda e: e.tensor_tensor(lbe[:], lbl[:], lbm[:].unsqueeze(2).to_broadcast([128, HH, DEPTH]), ALU.subtract),
             reads=[Tlbl, Tlbm], writes=[Tlbe])
        S.op("act", lambda e: e.activation(lbe[:], lbe[:], AF.Exp), reads=[Tlbe], writes=[Tlbe])
        S.op("dve", lambda e: e.tensor_reduce(lbm[:], lbe[:], mybir.AxisListType.X, ALU.add), reads=[Tlbe], writes=[Tlbm])
        S.op("dve", lambda e: e.reciprocal(lbm[:], lbm[:]), reads=[Tlbm], writes=[Tlbm])
        for li in range(L):
            S.op("dve", lambda e, li=li: e.tensor_tensor(
                lbt[:], lbe[:], lsel_sb[:, li * DEPTH:(li + 1) * DEPTH].unsqueeze(1).to_broadcast([128, HH, DEPTH]), ALU.mult),
                reads=[Tlbe, Tlsel], writes=[Tlbt])
            S.op("dve", lambda e, li=li: e.tensor_reduce(lbv[:, li, 0, :], lbt[:], mybir.AxisListType.X, ALU.add),
                 reads=[Tlbt], writes=[Tlb])
            S.op("dve", lambda e, li=li: e.tensor_tensor(lbv[:, li, 0, :], lbv[:, li, 0, :], lbm[:], ALU.mult),
                 reads=[Tlb, Tlbm], writes=[Tlb])
            S.op("dve", lambda e, li=li: e.tensor_scalar(lbv[:, li, 1, :], lbv[:, li, 0, :], -1.0, 1.0, ALU.mult, ALU.add),
                 reads=[Tlb], writes=[Tlb])
            S.op("dve", lambda e, li=li: e.tensor_scalar(lbv[:, li, 2, :], lbv[:, li, 1, :], -1.0, None, ALU.mult),
                 reads=[Tlb], writes=[Tlb])

        S.op("act", lambda e: e.activation(cact[:], c32[:], AF.Silu), reads=[Tc32], writes=[Tcact])
        for li in range(L):
            mps = P[7]
            for ci in range(N_ADA_T):
                s = ring_load(wada[li, ci])
                for k in range(KC):
                    S.op("pe", lambda e, s=s, k=k, ci=ci: e.matmul(
                        mps[:, ci * NSEQ:(ci + 1) * NSEQ], ring[:, s, k * 128:(k + 1) * 128], cact[:, k, :],
                        start=(k == 0), stop=(k == KC - 1)),
                        reads=[RT[s], Tcact], writes=[PT[7]])
            mv = modv[:, li].rearrange("p a b m s -> p (a b m) s")
            for b in range(NSEQ):
                S.op("dve", lambda e, b=b, li=li, mv=mv: e.tensor_tensor(
                    mv[:, :, b], mps[:, 0:N_ADA_T * NSEQ].rearrange("p (c s) -> p c s", s=NSEQ)[:, :, b],
                    bada_sb[:, li, :], ALU.add), reads=[PT[7], Tbada], writes=[Tmod])
            for sub in range(3):
                for b in range(NSEQ):
                    S.op("dve", lambda e, sub=sub, b=b, li=li: e.scalar_tensor_tensor(
                        modv[:, li, sub, 1, :, b], modv[:, li, sub, 1, :, b], 1.0, ng_sb[:, li, sub * 16:(sub + 1) * 16],
                        ALU.add, ALU.mult), reads=[Tmod, Tng], writes=[Tmod])
                if sub != 1:
                    S.op("dve", lambda e, sub=sub, li=li: e.tensor_scalar(
                        modv[:, li, sub, 2], modv[:, li, sub, 2], 0.5, None, ALU.mult), reads=[Tmod], writes=[Tmod])

        def rstd_from(ps_idx, eps_col=eps_ap):
            S.op("act", lambda e: e.activation(rstd[:], P[ps_idx][:], AF.Ln, bias=eps_col, scale=1.0),
                 reads=[PT[ps_idx], Tc], writes=[Trstd])
            S.op("act", lambda e: e.activation(rstd[:], rstd[:], AF.Exp, scale=-0.5), reads=[Trstd], writes=[Trstd])

        def norm_mod(li, sub, b):
            for k in range(KC):
                S.op("act", lambda e, k=k: e.activation(sq[:, k % 2, :], x_sb[:, k, :], AF.Square),
                     reads=[X[k]], writes=[Tsq[k % 2]])
                S.op("pe", lambda e, k=k: e.matmul(P[6][:], onesd[:, 0, :], sq[:, k % 2, :], start=(k == 0), stop=(k == KC - 1)),
                     reads=[Tsq[k % 2], Tones], writes=[PT[6]])
            rstd_from(6)
            for k in range(KC):
                S.op("dve", lambda e, k=k: e.scalar_tensor_tensor(
                    tmpa[:, k % 2, :], x_sb[:, k, :], modv[:, li, sub, 1, k, b:b + 1], rstd[:], ALU.mult, ALU.mult),
                    reads=[X[k], Tmod, Trstd], writes=[Ttmp[k % 2]])
                S.op("act", lambda e, k=k: e.activation(h_sb[:, k, :], tmpa[:, k % 2, :], AF.Identity,
                                                        bias=modv[:, li, sub, 0, k, b:b + 1], scale=1.0),
                     reads=[Ttmp[k % 2], Tmod], writes=[H[k]])

        def ffn(li, which, b):
            sub = 0 if which == 0 else 2
            base = 0 if which == 0 else N_FFN_T + N_MIX_T
            norm_mod(li, sub, b)
            ti = base
            for f in range(FC):
                sg_ = ring_load(wst[li, ti]); su_ = ring_load(wst[li, ti + 1]); ti += 2
                gp, up = (f % 2), 2 + (f % 2)
                for k in range(KC):
                    S.op("pe", lambda e, s=sg_, k=k, gp=gp: e.matmul(
                        P[gp][:], ring[:, s, k * 128:(k + 1) * 128], h_sb[:, k, :], start=(k == 0), stop=(k == KC - 1)),
                        reads=[RT[sg_], H[k]], writes=[PT[gp]])
                for k in range(KC):
                    S.op("pe", lambda e, s=su_, k=k, up=up: e.matmul(
                        P[up][:], ring[:, s, k * 128:(k + 1) * 128], h_sb[:, k, :], start=(k == 0), stop=(k == KC - 1)),
                        reads=[RT[su_], H[k]], writes=[PT[up]])
                S.op("act", lambda e, f=f, gp=gp: e.activation(sgt[:, f % 2, :], P[gp][:], AF.Silu),
                     reads=[PT[gp]], writes=[Tsg[f % 2]])
                S.op("dve", lambda e, f=f, up=up: e.tensor_tensor(scr[:, f, :], sgt[:, f % 2, :], P[up][:], ALU.mult),
                     reads=[Tsg[f % 2], PT[up]], writes=[HID[f]])
            for m in range(KC):
                yp = 4 + (m % 2)
                for q in range(4):
                    s = ring_load(wst[li, ti], 11 * 128); ti += 1
                    for f2 in range(11):
                        f = q * 11 + f2
                        S.op("pe", lambda e, s=s, f2=f2, f=f, yp=yp: e.matmul(
                            P[yp][:], ring[:, s, f2 * 128:(f2 + 1) * 128], scr[:, f, :], start=(f == 0), stop=(f == FC - 1)),
                            reads=[RT[s], HID[f]], writes=[PT[yp]])
                S.op("dve", lambda e, m=m, yp=yp: e.scalar_tensor_tensor(
                    x_sb[:, m, :], P[yp][:], modv[:, li, sub, 2, m, b:b + 1], x_sb[:, m, :], ALU.mult, ALU.add),
                    reads=[PT[yp], Tmod, X[m]], writes=[X[m]])

        def rope_tables(b, j):
            S.dma("sp", posi[0:64, :], posd[b:b + 1, j * TT:(j + 1) * TT].partition_broadcast(64), writes=[Tposi])
            S.op("dve", lambda e: e.tensor_copy(ang[0:64, :], posi[0:64, :]), reads=[Tposi], writes=[Tang])
            S.op("dve", lambda e: e.tensor_scalar(ang[0:64, :], ang[0:64, :], cst[0:64, C_INV:C_INV + 1], None, ALU.mult),
                 reads=[Tang, Tc], writes=[Tang])
            for (dst, Td, phase, sc_col, bi_col) in ((ropeC, TrC, math.pi / 2, C_ONE, C_NPI), (ropeS, TrS, 0.0, C_SGN, C_SGNB)):
                d = dst[0:64, :]
                kf_ = angk[0:64, :]
                ki_ = angi[0:64, :]
                S.op("dve", lambda e, d=d, phase=phase: e.tensor_scalar(d, ang[0:64, :], phase + math.pi, None, ALU.add),
                     reads=[Tang], writes=[Td])
                S.op("dve", lambda e, d=d: e.tensor_scalar(kf_, d, 1.0 / (2 * math.pi), None, ALU.mult), reads=[Td], writes=[Tangk])
                S.op("dve", lambda e: e.tensor_copy(ki_, kf_), reads=[Tangk], writes=[Tangi])
                S.op("dve", lambda e: e.tensor_copy(kf_, ki_), reads=[Tangi], writes=[Tangk])
                S.op("dve", lambda e, d=d: e.scalar_tensor_tensor(d, kf_, -6.28125, d, ALU.mult, ALU.add), reads=[Tangk, Td], writes=[Td])
                S.op("dve", lambda e, d=d: e.scalar_tensor_tensor(d, kf_, -0.0019353071795864769, d, ALU.mult, ALU.add),
                     reads=[Tangk, Td], writes=[Td])
                S.op("dve", lambda e, d=d: e.tensor_scalar(kf_, d, 0.0, 2 * math.pi, ALU.is_lt, ALU.mult), reads=[Td], writes=[Tangk])
                S.op("dve", lambda e, d=d: e.tensor_tensor(d, d, kf_, ALU.add), reads=[Tangk, Td], writes=[Td])
                S.op("dve", lambda e, d=d: e.tensor_scalar(kf_, d, 2 * math.pi, -2 * math.pi, ALU.is_ge, ALU.mult), reads=[Td], writes=[Tangk])
                S.op("dve", lambda e, d=d: e.tensor_tensor(d, d, kf_, ALU.add), reads=[Tangk, Td], writes=[Td])
                S.op("act", lambda e, d=d, sc_col=sc_col, bi_col=bi_col: e.activation(
                    d, d, AF.Sin, bias=cst[0:64, bi_col:bi_col + 1], scale=cst[0:64, sc_col:sc_col + 1]),
                    reads=[Td, Tc], writes=[Td])

        def mm_chunk(ps_idx, slot, rows=128, col0=0, ncol=128, rhs_tiles=None):
            for k in range(KC):
                S.op("pe", lambda e, k=k: e.matmul(
                    P[ps_idx][0:rows, :], ring[:, slot, k * 128 + col0:k * 128 + col0 + ncol], h_sb[:, k, :],
                    start=(k == 0), stop=(k == KC - 1)), reads=[RT[slot], H[k]], writes=[PT[ps_idx]])

        def mixer(li, b, j):
            ti = N_FFN_T
            sub = 1
            tcols = slice(j * TT, (j + 1) * TT)
            if j > 0:
                prev = list(range(j))
                dsrc = [t_ for p in prev for t_ in DCK[li][p]]
                for c in range(2):
                    S.dma("sp", ckvT[:, c, 0:j * TT], ckv_d[li, c, :, 0:j * TT], reads=dsrc, writes=[CKV[c][p] for p in prev])
                S.dma("sp", ckvtok[:, 0:4 * j, :], ckt_d[li, 0:4 * j].rearrange("n p c -> p n c"), reads=dsrc,
                      writes=[CKT[p] for p in prev])
                S.dma("sp", kpeT[0:64, 0:j * TT], kpe_d[li, :, 0:j * TT], reads=dsrc, writes=[KPE[p] for p in prev])
                S.dma("sp", S32[:].rearrange("p h d -> p (h d)"), s_d[li], reads=[DS[li]], writes=ST32)
                S.op("act", lambda e: e.activation(Sbf[:], S32[:], AF.Copy), reads=ST32, writes=STB)
            norm_mod(li, sub, b)
            for cb in range(2):
                for kq in range(4):
                    s_ = ring_load(wst[li, ti]); ti += 1
                    for k2 in range(4):
                        k = kq * 4 + k2
                        for tb in range(4):
                            S.op("pe", lambda e, s_=s_, k=k, k2=k2, tb=tb: e.matmul(
                                P[tb][:], h_sb[:, k, tb * 128:(tb + 1) * 128], ring[:, s_, k2 * 512:(k2 + 1) * 512],
                                start=(k == 0), stop=(k == KC - 1)), reads=[RT[s_], H[k]], writes=[PT[tb]])
                for tb in range(4):
                    S.op("act", lambda e, tb=tb, cb=cb: e.activation(vtok[:, tb, cb * 512:(cb + 1) * 512], P[tb][:], AF.Copy),
                         reads=[PT[tb]], writes=[VT[tb]])
            for c in range(4):
                s_ = ring_load(wst[li, ti]); ti += 1
                mm_chunk(c, s_)
                S.op("act", lambda e, c=c: e.activation(ft[:, c, :], P[c][:], AF.Copy), reads=[PT[c]], writes=[FT[c]])
                S.op("act", lambda e, c=c: e.activation(sq[:, c % 2, :], ft[:, c, :], AF.Square), reads=[FT[c]], writes=[Tsq[c % 2]])
                S.op("pe", lambda e, c=c: e.matmul(P[6][:], onesd[:, 1, :], sq[:, c % 2, :], start=(c == 0), stop=(c == 3)),
                     reads=[Tsq[c % 2], Tones], writes=[PT[6]])
            rstd_from(6)
            for c in range(4):
                S.op("dve", lambda e, c=c: e.scalar_tensor_tensor(
                    cq[:, c, :], ft[:, c, :], small_sb[:, li, c:c + 1], rstd[:], ALU.mult, ALU.mult),
                    reads=[FT[c], Tsmall, Trstd], writes=[CQ[c]])
            for c in range(2):
                s_ = ring_load(wst[li, ti]); ti += 1
                mm_chunk(4 + c, s_)
                S.op("act", lambda e, c=c: e.activation(ft[:, 4 + c, :], P[4 + c][:], AF.Copy), reads=[PT[4 + c]], writes=[FT[4 + c]])
                S.op("act", lambda e, c=c: e.activation(sq[:, c % 2, :], ft[:, 4 + c, :], AF.Square), reads=[FT[4 + c]], writes=[Tsq[c % 2]])
                S.op("pe", lambda e, c=c: e.matmul(P[6][:], onesd[:, 2, :], sq[:, c % 2, :], start=(c == 0), stop=(c == 1)),
                     reads=[Tsq[c % 2], Tones], writes=[PT[6]])
            rstd_from(6)
            for c in range(2):
                S.op("dve", lambda e, c=c: e.scalar_tensor_tensor(
                    ckvT[:, c, tcols], ft[:, 4 + c, :], small_sb[:, li, 4 + c:5 + c], rstd[:], ALU.mult, ALU.mult),
                    reads=[FT[4 + c], Tsmall, Trstd], writes=[CKV[c][j]])
            for c in range(2):
                for tb in range(4):
                    S.op("pe", lambda e, c=c, tb=tb: e.transpose(
                        P7b[:, (c * 4 + tb) * 128:(c * 4 + tb + 1) * 128],
                        ckvT[:, c, j * TT + tb * 128: j * TT + (tb + 1) * 128], ident),
                        reads=[CKV[c][j], Tcb], writes=[PT[7]])
            for c in range(2):
                S.op("act", lambda e, c=c: e.activation(
                    ckvtok[:, j * 4:(j + 1) * 4, c * 128:(c + 1) * 128],
                    P7b[:, c * 512:(c + 1) * 512].rearrange("p (tb d) -> p tb d", tb=4), AF.Copy),
                    reads=[PT[7]], writes=[CKT[j]])
            s_ = ring_load(wst[li, ti]); ti += 1
            mm_chunk(0, s_, rows=64, col0=0, ncol=64)
            mm_chunk(1, s_, rows=64, col0=64, ncol=64)
            S.op("dve", lambda e: e.tensor_tensor(tmpa[0:64, 0, :], P[0][0:64, :], ropeC[0:64, :], ALU.mult),
                 reads=[PT[0], TrC], writes=[Ttmp[0]])
            S.op("dve", lambda e: e.tensor_tensor(tmpa[0:64, 1, :], P[1][0:64, :], ropeS[0:64, :], ALU.mult),
                 reads=[PT[1], TrS], writes=[Ttmp[1]])
            S.op("dve", lambda e: e.tensor_tensor(kpeT[0:64, tcols], tmpa[0:64, 0, :], tmpa[0:64, 1, :], ALU.add),
                 reads=[Ttmp[0], Ttmp[1]], writes=[KPE[j]])
            if j < ntiles - 1:
                for c in range(2):
                    S.dma("sp", ckv_d[li, c, :, tcols], ckvT[:, c, tcols], reads=[CKV[c][j]], writes=[DCK[li][j][c]])
                S.dma("sp", ckt_d[li, 4 * j:4 * j + 4].rearrange("n p c -> p n c"), ckvtok[:, 4 * j:4 * j + 4, :],
                      reads=[CKT[j]], writes=[DCK[li][j][2]])
                S.dma("sp", kpe_d[li, :, tcols], kpeT[0:64, tcols], reads=[KPE[j]], writes=[DCK[li][j][3]])
            S.dma("pool", wkeep[:, 0, :], wst[li, ti], writes=[TWK[0]], sem="wk0"); ti += 1
            S.dma("pool", wkeep[:, 1, :], wst[li, ti], writes=[TWK[1]], sem="wk1"); ti += 1
            nkb = 4 * (j + 1)
            s_wq = None
            if j == 0:
                S.op("dve", lambda e: e.memset(S32[:], 0.0), writes=ST32)
                S.op("dve", lambda e: e.memset(Sbf[:], 0.0), writes=STB)
            for hd in range(HH):
                if hd % 2 == 0:
                    s_wq = ring_load(wst[li, ti]); ti += 1
                off = (hd % 2) * 256
                for (pi_, c0, nco, rows) in ((0, off, 128, 128), (1, off + 128, 64, 64), (2, off + 192, 64, 64)):
                    for k in range(4):
                        S.op("pe", lambda e, k=k, pi_=pi_, c0=c0, nco=nco, rows=rows, s_wq=s_wq: e.matmul(
                            P[pi_][0:rows, :], ring[:, s_wq, k * 512 + c0:k * 512 + c0 + nco], cq[:, k, :],
                            start=(k == 0), stop=(k == 3)), reads=[RT[s_wq], CQ[k]], writes=[PT[pi_]])
                S.op("act", lambda e: e.activation(qn, P[0][:], AF.Copy), reads=[PT[0]], writes=[Tqn])
                S.op("dve", lambda e: e.tensor_tensor(tmpa[0:64, 0, :], P[1][0:64, :], ropeC[0:64, :], ALU.mult),
                     reads=[PT[1], TrC], writes=[Ttmp[0]])
                S.op("dve", lambda e: e.tensor_tensor(tmpa[0:64, 1, :], P[2][0:64, :], ropeS[0:64, :], ALU.mult),
                     reads=[PT[2], TrS], writes=[Ttmp[1]])
                S.op("dve", lambda e: e.tensor_tensor(qpe[0:64, :], tmpa[0:64, 0, :], tmpa[0:64, 1, :], ALU.add),
                     reads=[Ttmp[0], Ttmp[1]], writes=[Tqpe])
                for cc in range(2):
                    S.op("pe", lambda e, cc=cc, hd=hd: e.matmul(
                        P[3 + cc][:], wkeep[:, 0, hd * 256 + cc * 128:hd * 256 + (cc + 1) * 128], qn, start=True, stop=True),
                        reads=[TWK[0], Tqn], writes=[PT[3 + cc]])
                    S.op("act", lambda e, cc=cc: e.activation(qp[:, cc, :], P[3 + cc][:], AF.Copy), reads=[PT[3 + cc]], writes=[QP[cc]])
                s_q = ring_load(wst[li, ti]); s_f = ring_load(wst[li, ti + 1]); s_g = ring_load(wst[li, ti + 2]); ti += 3
                mm_chunk(0, s_q); mm_chunk(1, s_f); mm_chunk(2, s_g)
                lb_ap = lbv[:, li, 0, hd:hd + 1]; oml_ap = lbv[:, li, 1, hd:hd + 1]; noml_ap = lbv[:, li, 2, hd:hd + 1]
                qf, ef, lf, kf, bt, gs = [ft[:, i, :] for i in range(6)]
                S.op("act", lambda e: e.activation(qf, P[0][:], AF.Silu), reads=[PT[0]], writes=[FT[0]])
                S.op("act", lambda e: e.activation(gs, P[2][:], AF.Silu), reads=[PT[2]], writes=[FT[5]])
                S.op("act", lambda e: e.activation(ef, P[1][:], AF.Exp, scale=-1.0), reads=[PT[1]], writes=[FT[1]])
                S.op("dve", lambda e: e.tensor_scalar(ef, ef, 1.0, None, ALU.add), reads=[FT[1]], writes=[FT[1]])
                S.op("dve", lambda e: e.reciprocal(ef, ef), reads=[FT[1]], writes=[FT[1]])
                S.op("act", lambda e, oml_ap=oml_ap, lb_ap=lb_ap: e.activation(lf, ef, AF.Ln, bias=lb_ap, scale=oml_ap),
                     reads=[FT[1], Tlb], writes=[FT[2]])
                S.op("dve", lambda e, oml_ap=oml_ap, noml_ap=noml_ap: e.tensor_scalar(kf, ef, noml_ap, oml_ap, ALU.mult, ALU.add),
                     reads=[FT[1], Tlb], writes=[FT[3]])
                S.op("dve", lambda e: e.tensor_tensor_scan(bt, cst[:, C_SM:C_SM + TT], lf, 0.0, ALU.mult, ALU.add),
                     reads=[FT[2], Tc], writes=[FT[4]])
                S.op("act", lambda e: e.activation(ef, bt, AF.Exp), reads=[FT[4]], writes=[FT[1]])
                S.op("act", lambda e: e.activation(lf, bt, AF.Exp, scale=-1.0), reads=[FT[4]], writes=[FT[2]])
                S.op("act", lambda e, hd=hd: e.activation(dcs[:, hd, :], ft[:, 4, 31::32], AF.Exp), reads=[FT[4]], writes=[DC[hd]])
                S.op("dve", lambda e: e.tensor_tensor(qin, qf, ef, ALU.mult), reads=[FT[0], FT[1]], writes=[Tqin])
                S.op("dve", lambda e: e.tensor_tensor(kf, kf, lf, ALU.mult), reads=[FT[3], FT[2]], writes=[FT[3]])
                S.op("act", lambda e: e.activation(kin, kf, AF.Copy), reads=[FT[3]], writes=[Tkin])
                S.op("dve", lambda e, hd=hd: e.tensor_tensor(
                    kout.rearrange("p (c t) -> p c t", t=32), ft[:, 3, :].rearrange("p (c t) -> p c t", t=32),
                    dcs[:, hd, :].unsqueeze(2).to_broadcast([128, 16, 32]), ALU.mult),
                    reads=[FT[3], DC[hd]], writes=[Tkout])
                for tb in range(4):
                    S.op("pe", lambda e, tb=tb: e.transpose(P7b[:, tb * 128:(tb + 1) * 128], kout[:, tb * 128:(tb + 1) * 128], ident),
                         reads=[Tkout, Tcb], writes=[PT[7]])
                for c in range(4):
                    S.op("act", lambda e, c=c: e.activation(
                        kexp[:, :, c, :], P7b[:, 0:512].rearrange("p (tb d) -> p tb d", tb=4), AF.Identity,
                        scale=cst[:, C_CM + c:C_CM + c + 1]), reads=[PT[7], Tc], writes=KEX)
                for tb in range(4):
                    S.op("pe", lambda e, tb=tb: e.matmul(
                        P[2][:, tb * 128:(tb + 1) * 128], kin[:, tb * 128:(tb + 1) * 128], qin[:, tb * 128:(tb + 1) * 128],
                        start=True, stop=True), reads=[Tkin, Tqin], writes=[PT[2]])
                S.op("dve", lambda e: e.tensor_tensor(atm, P[2][:], cst[:, C_BD:C_BD + TT], ALU.mult), reads=[PT[2], Tc], writes=[Tatm])

                def attn_step(kb, hd=hd):
                    kbi = kb - 4 * j
                    q0 = max(0, kbi) * 128
                    sp_ = 3 + (kb % 2)
                    pj = kb // 4
                    pb = kb % 2
                    kc = slice(kb * 128, (kb + 1) * 128)
                    S.op("pe", lambda e: e.matmul(P[sp_][:, q0:], ckvT[:, 0, kc], qp[:, 0, q0:], start=True, stop=False),
                         reads=[CKV[0][pj], QP[0]], writes=[PT[sp_]])
                    S.op("pe", lambda e: e.matmul(P[sp_][:, q0:], ckvT[:, 1, kc], qp[:, 1, q0:], start=False, stop=False),
                         reads=[CKV[1][pj], QP[1]], writes=[PT[sp_]])
                    S.op("pe", lambda e: e.matmul(P[sp_][:, q0:], kpeT[0:64, kc], qpe[0:64, q0:], start=False, stop=True),
                         reads=[KPE[pj], Tqpe], writes=[PT[sp_]])
                    S.op("act", lambda e: e.activation(pT[:, pb, q0:], P[sp_][:, q0:], AF.Exp, scale=SCALE),
                         reads=[PT[sp_]], writes=[PTT[pb]])
                    if kbi >= 0:
                        S.op("dve", lambda e: e.tensor_tensor(
                            pT[:, pb, q0:q0 + 128], pT[:, pb, q0:q0 + 128], cb16[:, C_TRI:C_TRI + 128], ALU.mult),
                            reads=[PTT[pb], Tcb], writes=[PTT[pb]])
                    S.op("pe", lambda e: e.matmul(P[5][:, q0:], ones1[:], pT[:, pb, q0:], start=(kb == 0), stop=(kb == nkb - 1)),
                         reads=[PTT[pb], Tones], writes=[PT[5]])
                    for cc in range(2):
                        S.op("pe", lambda e, cc=cc: e.matmul(
                            P[6 + cc][:, q0:], ckvtok[:, kb, cc * 128:(cc + 1) * 128], pT[:, pb, q0:],
                            start=(kb == 0), stop=(kb == nkb - 1)),
                            reads=[PTT[pb], CKT[pj]], writes=[PT[6 + cc]])

                def hgrn_step(ch, hd=hd):
                    tb, c = ch // 4, ch % 4
                    if c == 0:
                        S.op("pe", lambda e: e.matmul(
                            P[0][:, tb * 128:(tb + 1) * 128], vtok[:, tb, hd * 128:(hd + 1) * 128], atm[:, tb * 128:(tb + 1) * 128],
                            start=True, stop=False), reads=[VT[tb], Tatm], writes=[PT[0]])
                    S.op("pe", lambda e: e.matmul(
                        P[0][:, ch * 32:(ch + 1) * 32], Sbf[:, hd, :], qin[:, ch * 32:(ch + 1) * 32],
                        start=False, stop=(c == 3)), reads=[STB[hd], Tqin], writes=[PT[0]])
                    S.op("pe", lambda e: e.matmul(
                        P[1][:, c * 128:(c + 1) * 128], kexp[:, tb, c, :], vtok[:, tb, hd * 128:(hd + 1) * 128],
                        start=True, stop=True), reads=[KEX[tb], VT[tb]], writes=[PT[1]])
                    S.op("dve", lambda e: e.scalar_tensor_tensor(
                        S32[:, hd, :], S32[:, hd, :], dcs[:, hd, ch:ch + 1], P[1][:, c * 128:(c + 1) * 128],
                        ALU.mult, ALU.add), reads=[ST32[hd], DC[hd], PT[1]], writes=[ST32[hd]])
                    S.op("act", lambda e: e.activation(Sbf[:, hd, :], S32[:, hd, :], AF.Copy),
                         reads=[ST32[hd]], writes=[STB[hd]])

                done_kb = 0
                for ch in range(16):
                    hgrn_step(ch)
                    tgt = ((ch + 1) * nkb) // 16
                    while done_kb < tgt:
                        attn_step(done_kb)
                        done_kb += 1
                assert done_kb == nkb
                S.op("act", lambda e: e.activation(sq[:, 0, :], P[0][:], AF.Square), reads=[PT[0]], writes=[Tsq[0]])
                S.op("pe", lambda e: e.matmul(P[2][:], onesd[:, 3, :], sq[:, 0, :], start=True, stop=True),
                     reads=[Tsq[0], Tones], writes=[PT[2]])
                rstd_from(2)
                S.op("dve", lambda e: e.scalar_tensor_tensor(on_t[:], P[0][:], small_sb[:, li, 6:7], rstd[:], ALU.mult, ALU.mult),
                     reads=[PT[0], Tsmall, Trstd], writes=[Ton])
                S.op("dve", lambda e, hd=hd: e.tensor_tensor(scr[:, hd, :], on_t[:], gs, ALU.mult),
                     reads=[Ton, FT[5]], writes=[OT[hd]])
                S.op("dve", lambda e: e.reciprocal(rl[:], P[5][:]), reads=[PT[5]], writes=[Trl])
                for cc in range(2):
                    S.op("act", lambda e, cc=cc: e.activation(ac[:, cc, :], P[6 + cc][:], AF.Copy), reads=[PT[6 + cc]], writes=[AC[cc]])
                for cc in range(2):
                    S.op("pe", lambda e, cc=cc, hd=hd: e.matmul(
                        P[2][:], wkeep[:, 1, cc * 1024 + hd * 128:cc * 1024 + (hd + 1) * 128], ac[:, cc, :],
                        start=(cc == 0), stop=(cc == 1)), reads=[TWK[1], AC[cc]], writes=[PT[2]])
                S.op("dve", lambda e, hd=hd: e.tensor_tensor(scr[:, 8 + hd, :], P[2][:], rl[:], ALU.mult),
                     reads=[PT[2], Trl], writes=[OT[8 + hd]])
            if j < ntiles - 1:
                S.dma("sp", s_d[li], S32[:].rearrange("p h d -> p (h d)"), reads=ST32, writes=[DS[li]])
            for m in range(KC):
                s_ = ring_load(wst[li, ti]); ti += 1
                yp = m % 2
                for k in range(KC):
                    S.op("pe", lambda e, s_=s_, k=k, yp=yp: e.matmul(
                        P[yp][:], ring[:, s_, k * 128:(k + 1) * 128], scr[:, k, :], start=(k == 0), stop=(k == KC - 1)),
                        reads=[RT[s_], OT[k]], writes=[PT[yp]])
                S.op("dve", lambda e, m=m, yp=yp: e.scalar_tensor_tensor(
                    x_sb[:, m, :], P[yp][:], modv[:, li, sub, 2, m, b:b + 1], x_sb[:, m, :], ALU.mult, ALU.add),
                    reads=[PT[yp], Tmod, X[m]], writes=[X[m]])
            assert ti == N_FFN_T + N_MIX_T

        def final_norm():
            for k in range(KC):
                S.op("act", lambda e, k=k: e.activation(sq[:, k % 2, :], x_sb[:, k, :], AF.Square),
                     reads=[X[k]], writes=[Tsq[k % 2]])
                S.op("pe", lambda e, k=k: e.matmul(P[6][:], onesd[:, 0, :], sq[:, k % 2, :], start=(k == 0), stop=(k == KC - 1)),
                     reads=[Tsq[k % 2], Tones], writes=[PT[6]])
            rstd_from(6)
            for k in range(KC):
                S.op("dve", lambda e, k=k: e.scalar_tensor_tensor(
                    x_sb[:, k, :], x_sb[:, k, :], fng_sb[:, k:k + 1], rstd[:], ALU.mult, ALU.mult),
                    reads=[X[k], Tfng, Trstd], writes=[X[k]])

        for b in range(nseq):
            for j in range(ntiles):
                for kq in range(4):
                    S.dma("sp", x_sb[:, kq * 4:(kq + 1) * 4, :],
                          xT[b, kq * 512:(kq + 1) * 512, j * TT:(j + 1) * TT].rearrange("(k p) t -> p k t", p=128),
                          writes=X[kq * 4:(kq + 1) * 4])
                rope_tables(b, j)
                for li in range(L):
                    if "ffn1" in stages:
                        ffn(li, 0, b)
                        if BARRIER:
                            S.barrier()
                    if "mix" in stages:
                        mixer(li, b, j)
                        if BARRIER:
                            S.barrier()
                    if "ffn2" in stages:
                        ffn(li, 1, b)
                if do_final:
                    final_norm()
                for kq in range(4):
                    S.dma("sp", outT[b, kq * 512:(kq + 1) * 512, j * TT:(j + 1) * TT].rearrange("(k p) t -> p k t", p=128),
                          x_sb[:, kq * 4:(kq + 1) * 4, :], reads=X[kq * 4:(kq + 1) * 4])
        S.emit()
    return nc


def shared_inputs(inp, layers):
    L = len(layers)
    sh = {}
    sh["wst"] = np.stack([layer_stream(inp, l) for l in layers])
    sh["wada"] = np.stack([ada_stream(inp["w_ada"][l]) for l in layers])
    sh["bada"] = np.stack([np.ascontiguousarray(inp["b_ada"][l].reshape(N_ADA_T, 128).T) for l in layers])
    sh["ng"] = np.stack([np.ascontiguousarray(inp["norm_g"][l].reshape(48, 128).T) for l in layers])
    small = np.zeros((L, 128, 16), np.float32)
    for i, l in enumerate(layers):
        small[i, :, 0:4] = inp["qa_norm_g"][l].reshape(4, 128).T
        small[i, :, 4:6] = inp["kva_norm_g"][l].reshape(2, 128).T
        small[i, :, 6] = inp["hg_norm_g"][l]
    sh["small"] = small
    sh["lblog"] = np.ascontiguousarray(inp["hg_lb_logits"].reshape(DEPTH, HH, 128).transpose(2, 1, 0))
    lsel = np.zeros((128, L * DEPTH), np.float32)
    for i, l in enumerate(layers):
        lsel[:, i * DEPTH + 1: i * DEPTH + l + 1] = 1.0
    sh["lsel"] = lsel
    sh["fng"] = np.ascontiguousarray(inp["final_norm_g"].reshape(KC, 128).T)
    sh["cst"] = make_consts()
    return sh


def core_inputs(xT_all, inp, core, sh):
    b0 = core * NSEQ
    m = dict(sh)
    m["xT"] = xT_all[b0:b0 + NSEQ]
    m["cT"] = np.ascontiguousarray(inp["c"][b0:b0 + NSEQ].reshape(NSEQ, KC, 128).transpose(2, 1, 0))
    m["pos"] = np.ascontiguousarray(inp["positions"][b0:b0 + NSEQ]).astype(np.int32)
    return m


_PROG_CACHE = {}


def run_layers(xT_all, inp, layers, do_final, cores=None, **bk):
    key = (len(layers), do_final, tuple(sorted(bk.items())))
    if key not in _PROG_CACHE:
        _PROG_CACHE[key] = build_program(len(layers), do_final, **bk)
    nc = _PROG_CACHE[key]
    sh = shared_inputs(inp, layers)
    cores = list(range(NCORES)) if cores is None else cores
    in_maps = [core_inputs(xT_all, inp, c, sh) for c in cores]
    res = run_bass_kernel_spmd(nc, in_maps, core_ids=list(range(len(cores))))
    return np.concatenate([r["outT"] for r in res.results], axis=0)


def kernel(**inputs):
    inp = {k: np.asarray(v) for k, v in inputs.items()}
    xT = np.ascontiguousarray(inp["x"].astype(np.float32).transpose(0, 2, 1))
    xT = run_layers(xT, inp, list(range(DEPTH)), do_final=True)
    return np.ascontiguousarray(xT.transpose(0, 2, 1)).astype(np.float32)
```
